# Optimizing a Trainium2 kernel written in Bass

```python
import jax, jax.numpy as jnp
from jax import lax
import numpy as np

D_MODEL = 1024
BATCH = 2
SEQ = 8192
DEPTH = 1
DEC_BATCH = 128
DEC_SEQ = 4
PAST_LEN = 2048
PAGE_SIZE = 128

N_META = 16
A_HEADS = 8
A_HEAD_DIM = 64
A_WIDTH = A_HEADS * A_HEAD_DIM
B_HEADS = 4
B_KEY_DIM = 128
B_VAL_DIM = 128
B_KEY_WIDTH = B_HEADS * B_KEY_DIM
B_WIDTH = B_HEADS * B_VAL_DIM
MIX_WIDTH = A_WIDTH + B_WIDTH
IN_SIZES = (A_WIDTH, A_WIDTH, A_WIDTH, A_HEADS, A_WIDTH, B_KEY_WIDTH, B_KEY_WIDTH, B_WIDTH, B_WIDTH)
IN_WIDTH = sum(IN_SIZES)
BLOCK = 128
CHUNK = 128
RMS_EPS = 1e-6

kernel_name = "fox_hgrn2_parallel_heads_step"


def rmsnorm(x, g):
    xf = x.astype(jnp.float32)
    y = xf * lax.rsqrt(jnp.mean(xf * xf, axis=-1, keepdims=True) + RMS_EPS)
    return (y * g.astype(jnp.float32)).astype(x.dtype)


def branch_inputs(hn, w_in, b_forget, lb):
    bsz, t = hn.shape[:2]
    u = hn @ w_in
    offs = [int(o) for o in np.cumsum(IN_SIZES)[:-1]]
    qa, ka, va, fa, za, qb, fb, vb, zb = jnp.split(u, offs, axis=-1)
    qa = qa.reshape(bsz, t, A_HEADS, A_HEAD_DIM)
    ka = ka.reshape(bsz, t, A_HEADS, A_HEAD_DIM)
    va = va.reshape(bsz, t, A_HEADS, A_HEAD_DIM)
    logf_a = jax.nn.log_sigmoid(fa.astype(jnp.float32) + b_forget.astype(jnp.float32))
    lbh = lb.reshape(B_HEADS, B_KEY_DIM)
    g = lbh + (1.0 - lbh) * jax.nn.sigmoid(fb.astype(jnp.float32).reshape(bsz, t, B_HEADS, B_KEY_DIM))
    kb = 1.0 - g
    logf_b = jnp.log(g)
    qb = jax.nn.silu(qb.astype(jnp.float32)).reshape(bsz, t, B_HEADS, B_KEY_DIM)
    vb = vb.reshape(bsz, t, B_HEADS, B_VAL_DIM)
    return qa, ka, va, logf_a, za, qb, kb, vb, logf_b, zb


def fox_attend(q, cq, k, ck, v, mask):
    scale = A_HEAD_DIM ** -0.5
    s = jnp.einsum('bqhd,bkhd->bhqk', q, k).astype(jnp.float32) * scale
    s = s + jnp.transpose(cq, (0, 2, 1))[:, :, :, None] - jnp.transpose(ck, (0, 2, 1))[:, :, None, :]
    s = jnp.where(mask, s, -jnp.inf)
    p = jax.nn.softmax(s, axis=-1)
    return jnp.einsum('bhqk,bkhd->bqhd', p.astype(v.dtype), v)


def fox_prompt(q, k, v, logf):
    bsz, L = q.shape[:2]
    c = jnp.cumsum(logf, axis=1)
    meta_mask = jnp.tril(jnp.ones((N_META, N_META), dtype=bool))
    o_meta = fox_attend(q[:, :N_META], c[:, :N_META], k[:, :N_META], c[:, :N_META], v[:, :N_META], meta_mask)
    n_blk = (L - N_META) // BLOCK
    k_pos = jnp.arange(L)

    def one_block(b):
        start = N_META + b * BLOCK
        qblk = lax.dynamic_slice_in_dim(q, start, BLOCK, axis=1)
        cblk = lax.dynamic_slice_in_dim(c, start, BLOCK, axis=1)
        q_pos = start + jnp.arange(BLOCK)
        mask = k_pos[None, :] <= q_pos[:, None]
        return fox_attend(qblk, cblk, k, c, v, mask)

    o_blocks = lax.map(one_block, jnp.arange(n_blk))
    o_real = jnp.swapaxes(o_blocks, 0, 1).reshape(bsz, n_blk * BLOCK, A_HEADS, A_HEAD_DIM)
    return jnp.concatenate([o_meta, o_real], axis=1)


def fox_sample(q, k, v, logf, past_k, past_v, past_logf):
    c_past = jnp.cumsum(past_logf.astype(jnp.float32), axis=1)
    c_new = c_past[:, -1:] + jnp.cumsum(logf, axis=1)
    keys = jnp.concatenate([past_k.astype(k.dtype), k], axis=1)
    vals = jnp.concatenate([past_v.astype(v.dtype), v], axis=1)
    c_all = jnp.concatenate([c_past, c_new], axis=1)
    P, T = past_k.shape[1], q.shape[1]
    mask = jnp.arange(P + T)[None, :] <= (P + jnp.arange(T))[:, None]
    return fox_attend(q, c_new, keys, c_all, vals, mask)


def hgrn_chunk(S0, q, k, v, logf):
    S0 = S0.astype(jnp.float32)
    q = q.astype(jnp.float32)
    k = k.astype(jnp.float32)
    v = v.astype(jnp.float32)
    b = jnp.cumsum(logf.astype(jnp.float32), axis=1)
    o_inter = jnp.einsum('bthk,bhkv->bthv', q * jnp.exp(b), S0)
    C = q.shape[1]
    causal = jnp.tril(jnp.ones((C, C), dtype=bool))
    diff = b[:, :, None] - b[:, None, :]
    decay = jnp.exp(jnp.where(causal[None, :, :, None, None], diff, -jnp.inf))
    a = jnp.einsum('bthk,bshk,btshk->bhts', q, k, decay)
    o_intra = jnp.einsum('bhts,bshv->bthv', a, v)
    b_last = b[:, -1]
    S_new = jnp.exp(b_last)[..., None] * S0 + jnp.einsum('bshk,bshv->bhkv', k * jnp.exp(b_last[:, None] - b), v)
    return o_inter + o_intra, S_new


def hgrn_prompt(q, k, v, logf):
    bsz, L = q.shape[:2]
    S0 = jnp.zeros((bsz, B_HEADS, B_KEY_DIM, B_VAL_DIM), jnp.float32)
    o_meta, S = hgrn_chunk(S0, q[:, :N_META], k[:, :N_META], v[:, :N_META], logf[:, :N_META])
    n_c = (L - N_META) // CHUNK

    def to_chunks(a):
        return jnp.swapaxes(a[:, N_META:].reshape(bsz, n_c, CHUNK, *a.shape[2:]), 0, 1)

    def step(S, xs):
        o, S = hgrn_chunk(S, *xs)
        return S, o

    S, o_c = lax.scan(step, S, (to_chunks(q), to_chunks(k), to_chunks(v), to_chunks(logf)))
    o_real = jnp.swapaxes(o_c, 0, 1).reshape(bsz, n_c * CHUNK, B_HEADS, B_VAL_DIM)
    return jnp.concatenate([o_meta, o_real], axis=1), S


def merge_heads(o_a, za, o_b, zb, out_norm, w_out):
    bsz, t = o_a.shape[:2]
    ya = o_a.reshape(bsz, t, A_WIDTH) * jax.nn.silu(za)
    yb = rmsnorm(o_b, out_norm).reshape(bsz, t, B_WIDTH) * jax.nn.silu(zb)
    return jnp.concatenate([ya.astype(yb.dtype), yb], axis=-1) @ w_out


def setup_inputs(seed: int = 0) -> dict:
    key = jax.random.key(seed)
    ks = jax.random.split(key, 16)
    n_pages = PAST_LEN // PAGE_SIZE
    n_pool = (DEC_BATCH * n_pages * 5) // 4
    f32 = jnp.float32
    x_prompt = jax.random.normal(ks[0], (BATCH, SEQ, D_MODEL), f32)
    x_sample = jax.random.normal(ks[1], (DEC_BATCH, DEC_SEQ, D_MODEL), f32)
    cache_k = jax.random.normal(ks[2], (DEPTH, n_pool, PAGE_SIZE, A_HEADS, A_HEAD_DIM), f32)
    cache_v = jax.random.normal(ks[3], (DEPTH, n_pool, PAGE_SIZE, A_HEADS, A_HEAD_DIM), f32)
    cache_logf = jax.nn.log_sigmoid(3.0 + jax.random.normal(ks[4], (DEPTH, n_pool, PAGE_SIZE, A_HEADS), f32))
    state_hgrn = jax.random.normal(ks[5], (DEPTH, DEC_BATCH, B_HEADS, B_KEY_DIM, B_VAL_DIM), f32)
    perm = jax.random.permutation(ks[6], n_pool)
    page_table = perm[: DEC_BATCH * n_pages].reshape(DEC_BATCH, n_pages).astype(jnp.int32)
    meta_tokens = jax.random.normal(ks[7], (N_META, D_MODEL), f32)
    w_in = jax.random.normal(ks[8], (DEPTH, D_MODEL, IN_WIDTH), f32) * D_MODEL ** -0.5
    b_forget = 3.0 + 0.1 * jax.random.normal(ks[9], (DEPTH, A_HEADS), f32)
    hgrn_lower_bound = 0.1 * jax.random.normal(ks[10], (DEPTH + 1, B_KEY_WIDTH), f32)
    hgrn_out_norm = 1.0 + 0.01 * jax.random.normal(ks[11], (DEPTH, B_VAL_DIM), f32)
    pre_norm = 1.0 + 0.01 * jax.random.normal(ks[12], (DEPTH, D_MODEL), f32)
    post_norm = 1.0 + 0.01 * jax.random.normal(ks[13], (DEPTH, D_MODEL), f32)
    w_out = jax.random.normal(ks[14], (DEPTH, MIX_WIDTH, D_MODEL), f32) * MIX_WIDTH ** -0.5
    return {"x_prompt": x_prompt, "x_sample": x_sample, "cache_k": cache_k, "cache_v": cache_v,
            "cache_logf": cache_logf, "state_hgrn": state_hgrn, "page_table": page_table,
            "meta_tokens": meta_tokens, "w_in": w_in, "b_forget": b_forget,
            "hgrn_lower_bound": hgrn_lower_bound, "hgrn_out_norm": hgrn_out_norm,
            "pre_norm": pre_norm, "post_norm": post_norm, "w_out": w_out}


def reference(x_prompt, x_sample, cache_k, cache_v, cache_logf, state_hgrn, page_table,
              meta_tokens, w_in, b_forget, hgrn_lower_bound, hgrn_out_norm, pre_norm, post_norm, w_out):
    bp = x_prompt.shape[0]
    bs = x_sample.shape[0]
    n_pages = page_table.shape[1]
    page = cache_k.shape[2]
    meta = jnp.broadcast_to(meta_tokens[None].astype(x_prompt.dtype), (bp, N_META, D_MODEL))
    hp = jnp.concatenate([meta, x_prompt], axis=1)
    hs = x_sample
    lb_all = jnp.cumsum(jax.nn.softmax(hgrn_lower_bound.astype(jnp.float32), axis=0), axis=0)
    pk, pv, plf, pS, sk, sv, slf, sS = [], [], [], [], [], [], [], []
    for l in range(DEPTH):
        lb = lb_all[l]
        hn = rmsnorm(hp, pre_norm[l])
        qa, ka, va, lfa, za, qb, kb, vb, lfb, zb = branch_inputs(hn, w_in[l], b_forget[l], lb)
        oa = fox_prompt(qa, ka, va, lfa)
        ob, S_p = hgrn_prompt(qb, kb, vb, lfb)
        hp = hp + rmsnorm(merge_heads(oa, za, ob, zb, hgrn_out_norm[l], w_out[l]), post_norm[l])
        pk.append(ka); pv.append(va); plf.append(lfa); pS.append(S_p)
        hn = rmsnorm(hs, pre_norm[l])
        qa, ka, va, lfa, za, qb, kb, vb, lfb, zb = branch_inputs(hn, w_in[l], b_forget[l], lb)
        past_k = cache_k[l][page_table].reshape(bs, n_pages * page, A_HEADS, A_HEAD_DIM)
        past_v = cache_v[l][page_table].reshape(bs, n_pages * page, A_HEADS, A_HEAD_DIM)
        past_lf = cache_logf[l][page_table].reshape(bs, n_pages * page, A_HEADS)
        oa = fox_sample(qa, ka, va, lfa, past_k, past_v, past_lf)
        ob, S_s = hgrn_chunk(state_hgrn[l], qb, kb, vb, lfb)
        hs = hs + rmsnorm(merge_heads(oa, za, ob, zb, hgrn_out_norm[l], w_out[l]), post_norm[l])
        sk.append(ka); sv.append(va); slf.append(lfa); sS.append(S_s)
    y_prompt = hp[:, N_META:]
    return (y_prompt, hs, jnp.stack(pk), jnp.stack(pv), jnp.stack(plf), jnp.stack(pS),
            jnp.stack(sk), jnp.stack(sv), jnp.stack(slf), jnp.stack(sS))
```

```python
import contextlib
import os
import numpy as np
import ml_dtypes
import concourse.bass as bass
import concourse.mybir as mybir
from concourse.bass_utils import run_bass_kernel_spmd

F32 = mybir.dt.float32
BF16 = mybir.dt.bfloat16
I32 = mybir.dt.int32
AF = mybir.ActivationFunctionType
ALU = mybir.AluOpType
AX = mybir.AxisListType

D = 1024
NMETA = 16
EPS = 1e-6
SCALE = 64 ** -0.5


class Cfg:
    def __init__(self, nb=64, sb=16, npg=16, npool=2560):
        self.NB = nb
        self.SEQ = 128 * nb
        self.L = NMETA + self.SEQ
        self.NT = nb // 4
        self.TQ = self.SEQ // 4
        self.SB = sb
        self.NS = 4 * sb
        self.NPG = npg
        self.PAST = 128 * npg
        self.NPOOL = npool


class Trk:
    def __init__(self, nc, n_dma_sems=40):
        self.nc = nc
        self.engs = {"pe": nc.tensor, "act": nc.scalar, "dve": nc.vector, "pool": nc.gpsimd, "sp": nc.sync}
        self.sem, self.cnt, self.seen, self.last_w, self.readers = {}, {}, {}, {}, {}
        self._ctx = []
        for k in self.engs:
            cm = nc.semaphore("s_" + k)
            self.sem[k] = cm.__enter__()
            self._ctx.append(cm)
            self.cnt[k] = 0
        self.dma_sems = []
        for i in range(n_dma_sems):
            cm = nc.semaphore("s_dma%d" % i)
            s = cm.__enter__()
            self._ctx.append(cm)
            self.dma_sems.append([s, 0])
        self.dma_rr = 0
        cm = nc.semaphore("s_cc")
        self.cc_sem = cm.__enter__()
        self._ctx.append(cm)
        self.cc_cnt = 0
        self.n_wait = 0

    def close(self):
        for cm in reversed(self._ctx):
            cm.__exit__(None, None, None)

    def _wait(self, e, tok):
        if tok is None:
            return
        key, sem, c = tok
        if key == "pe" and e == "pe":
            return
        if self.seen.get((e, key), 0) >= c:
            return
        self.engs[e].wait_ge(sem, c)
        if os.environ.get("K_TRACE"):
            print("   WAIT", e, "on", key, c)
        self.n_wait += 1
        self.seen[(e, key)] = c

    def _deps(self, e, reads, writes):
        for b in reads:
            for t in self.last_w.get(b, ()):
                self._wait(e, t)
        for b in writes:
            for t in self.last_w.get(b, ()):
                self._wait(e, t)
            for t in self.readers.get(b, ()):
                self._wait(e, t)

    def _commit(self, tok, reads, writes):
        for b in reads:
            lst = self.readers.setdefault(b, [])
            lst[:] = [t for t in lst if t[0] != tok[0]] + [tok]
        for b in writes:
            lst = self.last_w.setdefault(b, [])
            lst[:] = [t for t in lst if t[0] != tok[0]] + [tok]
            self.readers[b] = []

    def _skip(self):
        self.n_emit = getattr(self, "n_emit", 0) + 1
        lim = int(os.environ.get("K_LIMIT", "0"))
        if os.environ.get("K_TRACE"):
            import traceback
            fr = traceback.extract_stack()[-3]
            print("OP", self.n_emit, fr.lineno, fr.line[:110])
        if str(self.n_emit) in os.environ.get("K_SKIP", "").split(","):
            return True
        return lim > 0 and self.n_emit > lim

    def op(self, e, fn, reads=(), writes=()):
        if self._skip():
            return None
        if e != "pe":
            writes = list(writes) + [b for b in reads if b.startswith("p") and b[1:2].isupper()]
        self._deps(e, reads, writes)
        inst = fn(self.engs[e])
        self.cnt[e] += 1
        inst.then_inc(self.sem[e], 1)
        tok = (e, self.sem[e], self.cnt[e])
        self._commit(tok, reads, writes)
        return tok

    def dma(self, e, fn, reads=(), writes=()):
        if self._skip():
            return None
        self._deps(e, reads, writes)
        i = self.dma_rr
        self.dma_rr = (self.dma_rr + 1) % len(self.dma_sems)
        ent = self.dma_sems[i]
        key = "dma%d" % i
        if ent[1] > 0:
            self._wait(e, (key, ent[0], ent[1]))
        inst = fn(self.engs[e])
        ent[1] += 16
        inst.then_inc(ent[0], 16)
        tok = (key, ent[0], ent[1])
        self._commit(tok, reads, writes)
        return tok

    def cc(self, fn, reads=(), writes=()):
        e = "pool"
        self._deps(e, reads, writes)
        if self.cc_cnt > 0:
            self._wait(e, ("cc", self.cc_sem, self.cc_cnt))
        inst = fn(self.engs[e])
        self.cc_cnt += 1
        inst.then_inc(self.cc_sem)
        tok = ("cc", self.cc_sem, self.cc_cnt)
        self._commit(tok, reads, writes)
        return tok

    def barrier(self):
        toks = [(k, self.sem[k], self.cnt[k]) for k in self.engs if self.cnt[k] > 0]
        toks += [("dma%d" % i, ent[0], ent[1]) for i, ent in enumerate(self.dma_sems) if ent[1] > 0]
        if self.cc_cnt:
            toks.append(("cc", self.cc_sem, self.cc_cnt))
        for e in self.engs:
            for t in toks:
                if t[0] != e:
                    self._wait(e, t)

    def finish(self, e="sp"):
        for i, ent in enumerate(self.dma_sems):
            if ent[1] > 0:
                self._wait(e, ("dma%d" % i, ent[0], ent[1]))
        if self.cc_cnt:
            self._wait(e, ("cc", self.cc_sem, self.cc_cnt))
        for k in self.engs:
            if k != e and self.cnt[k] > 0:
                self._wait(e, (k, self.sem[k], self.cnt[k]))


C_ID, C_TRI, C_TQ, C_TAUX, C_TM = 0, 128, 256, 384, 387
C_TQ16, C_TAUX16, C_TM16, C_SEL, C_ONES, C_END = 515, 531, 534, 550, 678, 806


def make_consts():
    c = np.zeros((128, C_END), np.float32)
    s = np.arange(128)[:, None]
    t = np.arange(128)[None, :]
    c[:, C_ID:C_ID + 128] = (s == t)
    c[:, C_TRI:C_TRI + 128] = (s <= t)
    c[:, C_TQ:C_TQ + 128] = (s <= t).astype(np.float32) - (s <= 63)
    c[:, C_TAUX + 0] = (s[:, 0] <= 63)
    c[:, C_TAUX + 1] = 1.0
    c[:, C_TAUX + 2] = (s[:, 0] > 63)
    c[:, C_TM:C_TM + 128] = (s <= 63).astype(np.float32) - (s <= t)
    s16 = np.arange(16)[:, None]
    t16 = np.arange(16)[None, :]
    c[:16, C_TQ16:C_TQ16 + 16] = (s16 <= t16).astype(np.float32) - (s16 <= 7)
    c[:16, C_TAUX16 + 0] = (s16[:, 0] <= 7)
    c[:16, C_TAUX16 + 1] = 1.0
    c[:16, C_TAUX16 + 2] = (s16[:, 0] > 7)
    c[:16, C_TM16:C_TM16 + 16] = (s16 <= 7).astype(np.float32) - (s16 <= t16)
    c[64, C_SEL:C_SEL + 64] = 1.0
    c[0, C_SEL + 64:C_SEL + 128] = 1.0
    c[:, C_ONES:C_ONES + 128] = 1.0
    return c


def consts2_layout(cfg):
    SB, NS = cfg.SB, cfg.NS
    lay, o = {}, 0
    for name, w in [("BT", 128), ("BU", 128), ("MNEW", SB * 32), ("BMT", SB * NS), ("BM", SB), ("SUP", 128), ("SELP", 4), ("GCOL", 1)]:
        lay[name] = o
        o += w
    lay["END"] = o
    return lay


def make_consts2(cfg):
    SB, NS = cfg.SB, cfg.NS
    lay = consts2_layout(cfg)
    c = np.zeros((128, lay["END"]), np.float32)
    t = np.arange(NS)
    same = (t[:, None] // 4) == (t[None, :] // 4)
    c[:NS, lay["BT"]:lay["BT"] + NS] = same & (t[:, None] <= t[None, :])
    c[:NS, lay["BU"]:lay["BU"] + NS] = same & (t[:, None] > t[None, :])
    m = np.zeros((NS, SB, 8, 4), np.float32)
    for tok in range(NS):
        b, jq = tok // 4, tok % 4
        m[tok, b, :, jq:] = 1.0
    c[:NS, lay["MNEW"]:lay["MNEW"] + SB * 32] = m.reshape(NS, SB * 32)
    bmt = np.zeros((SB, NS), np.float32)
    for b in range(SB):
        bmt[b, 4 * b:4 * b + 4] = 1.0
    c[:, lay["BMT"]:lay["BMT"] + SB * NS] = bmt.reshape(1, SB * NS)
    c[:NS, lay["BM"]:lay["BM"] + SB] = bmt.T
    p = np.arange(128)
    c[:, lay["SUP"]:lay["SUP"] + 128] = (p[:, None] > p[None, :])
    c[:, lay["SELP"]:lay["SELP"] + 4] = (p[:, None] // 32) == np.arange(4)[None, :]
    c[:, lay["GCOL"]] = p % 32
    return c


W_QA, W_KA, W_ZA, W_QB, W_ZB, W_T = 0, 128, 256, 384, 512, 640
WT_VA, WT_FA, WT_FB, WT_VB, WT_N = 0, 128, 130, 258, 386
WCORE_N = W_T + WT_N


def build_program(cfg, do_sample=True):
    nc = bass.Bass("TRN2", target_bir_lowering=False)
    L, SEQ, NB, NT, TQ = cfg.L, cfg.SEQ, cfg.NB, cfg.NT, cfg.TQ

    def din(name, shape, dt=F32):
        return nc.dram_tensor(name, list(shape), dt, kind="ExternalInput").ap()

    def dout(name, shape, dt=F32):
        return nc.dram_tensor(name, list(shape), dt, kind="ExternalOutput").ap()

    xp = din("xp", [SEQ, D])
    xq = din("xq", [TQ, D])
    meta = din("meta", [NMETA, D])
    wcore = din("wcore", [D, WCORE_N])
    wout = din("wout", [D, D])
    bf2 = din("bf2", [1, 2])
    hl_row = din("hl_row", [1, 256])
    gon = din("gon", [128, 1])
    gpre = din("gpre", [1, D])
    gpost = din("gpost", [1, D])
    cst_d = din("cst", [128, C_END])
    qidx = din("qidx", [128, 8], I32)

    y_q = dout("y_q", [TQ, D])
    pkT = dout("pkT", [128, L])
    pv = dout("pv", [L, 128])
    plf = dout("plf", [L, 2])
    pS = dout("pS", [128, 128])

    CH = min(2048, SEQ)
    NCH = SEQ // CH
    mix_src = nc.dram_tensor("mix_src", [NCH * 256, CH], BF16)
    mix_dst = nc.dram_tensor("mix_dst", [NCH * 1024, CH], BF16)

    def mix_src_ap(half, c0):
        ci, off = c0 // CH, c0 % CH
        return mix_src[ci * 256 + 128 * half:ci * 256 + 128 * half + 128, off:off + 512]

    c2 = consts2_layout(Cfg(nb=cfg.NB, sb=min(4, cfg.SB), npg=cfg.NPG, npool=cfg.NPOOL))
    es = contextlib.ExitStack()
    es.enter_context(nc.allow_non_contiguous_dma(reason="small strided parameter loads / column stores"))
    T = Trk(nc)

    sfx = [""]

    def sb(stack, name, shape, dt=F32):
        return stack.enter_context(nc.sbuf_tensor("sb_" + name + sfx[0], list(shape), dt))

    def ps(stack, name, shape, dt=F32):
        return stack.enter_context(nc.psum_tensor("ps_" + name + sfx[0], list(shape), dt))

    cst = sb(es, "cst", [128, C_END])
    identb = sb(es, "identb", [128, 128], BF16)
    onesb = sb(es, "onesb", [128, 128], BF16)
    pa = contextlib.ExitStack()
    KT = sb(pa, "KT", [128, L], BF16)
    QT = sb(pa, "QT", [128, L], BF16)
    zaT = sb(pa, "zaT", [128, SEQ], BF16)
    Vaug = sb(pa, "Vaug", [128, NB + 1, 2, 128], BF16)
    negc = sb(pa, "negc", [128, NB + 1, 2])
    rq = sb(pa, "rq", [128, NT + 1, 2])

    T.dma("sp", lambda e: e.dma_start(out=cst[:], in_=cst_d[:, :]), writes=["cst"])
    T.dma("pool", lambda e: e.dma_start(out=identb[:], in_=cst_d[:, C_ID:C_ID + 128]), writes=["identb"])
    T.op("dve", lambda e: e.memset(onesb[:], 1.0), writes=["onesb"])
    T.op("pool", lambda e: e.memset(Vaug[:], 0.0), writes=["Vaug"])
    T.op("pool", lambda e: e.memset(Vaug[:, :, 0, 64:65], 1.0), writes=["Vaug"])
    T.op("pool", lambda e: e.memset(Vaug[:, :, 1, 0:1], 1.0), writes=["Vaug"])
    tri = cst[:, C_TRI:C_TRI + 128]

    p1 = contextlib.ExitStack()
    wcb = sb(p1, "wcb", [128, 8, WCORE_N], BF16)
    gpre_bc = sb(p1, "gpre_bc", [128, D])
    bf_bc = sb(p1, "bf_bc", [128, 2])
    lb_bc = sb(p1, "lb_bc", [128, 128])
    oml_bc = sb(p1, "oml_bc", [128, 128])
    gon_sb = sb(p1, "gon_sb", [128, 1])
    hl_bc = sb(p1, "hl_bc", [128, 2, 128])
    S0 = sb(p1, "S0", [128, 128])
    S0t = sb(p1, "S0t", [128, 128])
    S0b = sb(p1, "S0b", [128, 128], BF16)
    ctot = sb(p1, "ctot", [128, 2])
    xblk = [sb(p1, "xblk%d" % i, [128, D]) for i in range(2)]
    junk = sb(p1, "junk", [128, D], BF16)
    ss = sb(p1, "ss", [128, 4])
    hn = [sb(p1, "hn%d" % i, [128, D], BF16) for i in range(2)]
    hnT = sb(p1, "hnT", [128, 8, 512], BF16)
    KTf = sb(p1, "KTf", [128, 512])
    vaf = sb(p1, "vaf", [128, 4, 128])
    qbT = sb(p1, "qbT", [128, 512])
    zbT = sb(p1, "zbT", [128, 512], BF16)
    vbb = sb(p1, "vbb", [128, 4, 128], BF16)
    fa_sb = sb(p1, "fa_sb", [128, 4, 2])
    lf_sb = sb(p1, "lf_sb", [128, 4, 2])
    sig = sb(p1, "sig", [128, 4, 128])
    gg = sb(p1, "gg", [128, 4, 128])
    logg = sb(p1, "logg", [128, 4, 128])
    kb = sb(p1, "kb", [128, 4, 128])
    EK = sb(p1, "EK", [128, 4, 128])
    EQ = sb(p1, "EQ", [128, 4, 128])
    e3 = sb(p1, "e3", [128, 4, 3])
    K2 = sb(p1, "K2", [128, 4, 128], BF16)
    K2T = sb(p1, "K2T", [128, 4, 128], BF16)
    Q2T = sb(p1, "Q2T", [128, 4, 128], BF16)
    ATm = sb(p1, "ATm", [128, 4, 128], BF16)
    oT = sb(p1, "oT", [128, 512])
    sq = sb(p1, "sq", [128, 512], BF16)
    rstd = sb(p1, "rstd", [128, 512])
    mixB = sb(p1, "mixB", [128, 512], BF16)
    pT = ps(p1, "pT", [128, 8, 128], BF16)
    pF = [ps(p1, "pF%d" % i, [128, 512]) for i in range(2)]
    pTf = [ps(p1, "pTf%d" % i, [128, 512]) for i in range(2)]
    pH1 = ps(p1, "pH1", [128, 4, 128])
    pH2 = ps(p1, "pH2", [128, 4, 128])
    pH3 = ps(p1, "pH3", [128, 256])
    pK2T = pT

    T.dma("pool", lambda e: e.dma_start(out=wcb[:], in_=wcore.rearrange("(c p) n -> p c n", p=128)), writes=["wcb"])
    T.dma("sp", lambda e: e.dma_start(out=gpre_bc[:], in_=gpre.partition_broadcast(128)), writes=["gpre_bc"])
    T.dma("sp", lambda e: e.dma_start(out=bf_bc[:], in_=bf2.partition_broadcast(128)), writes=["bf_bc"])
    T.dma("sp", lambda e: e.dma_start(out=gon_sb[:], in_=gon[:, :]), writes=["gon_sb"])
    T.dma("sp", lambda e: e.dma_start(out=hl_bc[:].rearrange("p a b -> p (a b)"), in_=hl_row.partition_broadcast(128)), writes=["hl_bc"])
    T.op("dve", lambda e: e.tensor_tensor(out=lb_bc[:], in0=hl_bc[:, 0, :], in1=hl_bc[:, 1, :], op=ALU.subtract), reads=["hl_bc"], writes=["lb_bc"])
    T.op("act", lambda e: e.activation(out=lb_bc[:], in_=lb_bc[:], func=AF.Sigmoid), reads=["lb_bc"], writes=["lb_bc"])
    T.op("dve", lambda e: e.tensor_scalar(out=oml_bc[:], in0=lb_bc[:], scalar1=-1.0, scalar2=1.0, op0=ALU.mult, op1=ALU.add), reads=["lb_bc"], writes=["oml_bc"])
    T.op("dve", lambda e: e.memset(S0[:], 0.0), writes=["S0"])
    T.op("dve", lambda e: e.memset(ctot[:], 0.0), writes=["ctot"])

    blk_i = [0]

    def tile_phase1(tt):
        if tt == 0:
            n, nblk, bs, tok0 = NMETA, 1, NMETA, 0
        else:
            n, nblk, bs, tok0 = 512, 4, 128, NMETA + 512 * (tt - 1)
        gb0 = 0 if tt == 0 else 1 + 4 * (tt - 1)
        for bl in range(nblk):
            k = blk_i[0] % 2
            blk_i[0] += 1
            xb, hb = xblk[k], hn[k]
            xbn, hbn = "xblk%d" % k, "hn%d" % k
            if tt == 0:
                src = meta[:, :]
            else:
                r0 = 512 * (tt - 1) + 128 * bl
                src = xp[r0:r0 + 128, :]
            T.dma("sp", lambda e: e.dma_start(out=xb[0:bs, :], in_=src), writes=[xbn])
            T.op("act", lambda e: e.activation(out=junk[0:bs, :], in_=xb[0:bs, :], func=AF.Square, accum_out=ss[0:bs, 0:1]),
                 reads=[xbn], writes=["junk", "ss"])
            T.op("act", lambda e: e.activation(out=ss[0:bs, 1:2], in_=ss[0:bs, 0:1], func=AF.Ln, bias=EPS, scale=1.0 / D), reads=["ss"], writes=["ss"])
            T.op("act", lambda e: e.activation(out=ss[0:bs, 2:3], in_=ss[0:bs, 1:2], func=AF.Exp, scale=-0.5), reads=["ss"], writes=["ss"])
            T.op("dve", lambda e: e.scalar_tensor_tensor(out=hb[0:bs, :], in0=xb[0:bs, :], scalar=ss[0:bs, 2:3], in1=gpre_bc[0:bs, :],
                                                         op0=ALU.mult, op1=ALU.mult), reads=[xbn, "ss", "gpre_bc"], writes=[hbn])
            for kc in range(8):
                T.op("pe", lambda e: e.transpose(pT[:, kc, 0:bs], hb[0:bs, kc * 128:(kc + 1) * 128], identb[0:bs, 0:bs]),
                     reads=[hbn, "identb"], writes=["pT"])
            T.op("act", lambda e: e.activation(out=hnT[:, :, bl * 128:bl * 128 + bs], in_=pT[:, :, 0:bs], func=AF.Copy), reads=["pT"], writes=["hnT"])
        fi = [0]

        def fform(c0):
            pf = pF[fi[0] % 2]
            nm = "pF%d" % (fi[0] % 2)
            fi[0] += 1
            for kc in range(8):
                T.op("pe", lambda e: e.matmul(pf[:, 0:n], lhsT=wcb[:, kc, c0:c0 + 128], rhs=hnT[:, kc, 0:n], start=(kc == 0), stop=(kc == 7)),
                     reads=["wcb", "hnT"], writes=[nm])
            return pf, nm

        pf, nm = fform(W_QA)
        T.op("dve", lambda e: e.tensor_copy(out=QT[:, tok0:tok0 + n], in_=pf[:, 0:n]), reads=[nm], writes=["QT"])
        pf, nm = fform(W_KA)
        T.op("act", lambda e: e.activation(out=KTf[:, 0:n], in_=pf[:, 0:n], func=AF.Copy), reads=[nm], writes=["KTf"])
        T.op("dve", lambda e: e.tensor_copy(out=KT[:, tok0:tok0 + n], in_=pf[:, 0:n]), reads=[nm], writes=["KT"])
        T.dma("sp", lambda e: e.dma_start(out=pkT[:, tok0:tok0 + n], in_=KTf[:, 0:n]), reads=["KTf"])
        if tt > 0:
            pf, nm = fform(W_ZA)
            T.op("act", lambda e: e.activation(out=zaT[:, tok0 - NMETA:tok0 - NMETA + n], in_=pf[:, 0:n], func=AF.Silu), reads=[nm], writes=["zaT"])
        pf, nm = fform(W_QB)
        T.op("act", lambda e: e.activation(out=qbT[:, 0:n], in_=pf[:, 0:n], func=AF.Silu), reads=[nm], writes=["qbT"])
        if tt > 0:
            pf, nm = fform(W_ZB)
            T.op("act", lambda e: e.activation(out=zbT[:, 0:n], in_=pf[:, 0:n], func=AF.Silu), reads=[nm], writes=["zbT"])
        for bl in range(nblk):
            pt_ = pTf[bl % 2]
            nm = "pTf%d" % (bl % 2)
            for kc in range(8):
                T.op("pe", lambda e: e.matmul(pt_[0:bs, 0:WT_N], lhsT=hnT[:, kc, bl * 128:bl * 128 + bs], rhs=wcb[:, kc, W_T:W_T + WT_N],
                                              start=(kc == 0), stop=(kc == 7)), reads=["hnT", "wcb"], writes=[nm])
            gb = gb0 + bl
            T.op("act", lambda e: e.activation(out=vaf[0:bs, bl, :], in_=pt_[0:bs, WT_VA:WT_VA + 128], func=AF.Copy), reads=[nm], writes=["vaf"])
            T.op("dve", lambda e: e.tensor_copy(out=Vaug[0:bs, gb, 0, 0:64], in_=pt_[0:bs, WT_VA:WT_VA + 64]), reads=[nm], writes=["Vaug"])
            T.op("dve", lambda e: e.tensor_copy(out=Vaug[0:bs, gb, 1, 64:128], in_=pt_[0:bs, WT_VA + 64:WT_VA + 128]), reads=[nm], writes=["Vaug"])
            T.op("dve", lambda e: e.tensor_tensor(out=fa_sb[0:bs, bl, :], in0=pt_[0:bs, WT_FA:WT_FA + 2], in1=bf_bc[0:bs, :], op=ALU.add),
                 reads=[nm, "bf_bc"], writes=["fa_sb"])
            T.op("act", lambda e: e.activation(out=sig[0:bs, bl, :], in_=pt_[0:bs, WT_FB:WT_FB + 128], func=AF.Sigmoid), reads=[nm], writes=["sig"])
            T.op("dve", lambda e: e.tensor_copy(out=vbb[0:bs, bl, :], in_=pt_[0:bs, WT_VB:WT_VB + 128]), reads=[nm], writes=["vbb"])
        if tt == 0:
            T.dma("sp", lambda e: e.dma_start(out=pv[0:NMETA, :], in_=vaf[0:NMETA, 0, :]), reads=["vaf"])
        else:
            T.dma("sp", lambda e: e.dma_start(out=pv[tok0:tok0 + n, :].rearrange("(b p) d -> p b d", p=128), in_=vaf[:, :, :]), reads=["vaf"])
        T.op("act", lambda e: e.activation(out=lf_sb[0:bs, 0:nblk, :], in_=fa_sb[0:bs, 0:nblk, :], func=AF.Exp, scale=-1.0), reads=["fa_sb"], writes=["lf_sb"])
        T.op("act", lambda e: e.activation(out=lf_sb[0:bs, 0:nblk, :], in_=lf_sb[0:bs, 0:nblk, :], func=AF.Ln, bias=1.0, scale=1.0), reads=["lf_sb"], writes=["lf_sb"])
        T.op("dve", lambda e: e.tensor_scalar(out=lf_sb[0:bs, 0:nblk, :], in0=lf_sb[0:bs, 0:nblk, :], scalar1=-1.0, scalar2=None, op0=ALU.mult),
             reads=["lf_sb"], writes=["lf_sb"])
        if tt == 0:
            T.dma("sp", lambda e: e.dma_start(out=plf[0:NMETA, :], in_=lf_sb[0:NMETA, 0, :]), reads=["lf_sb"])
        else:
            T.dma("sp", lambda e: e.dma_start(out=plf[tok0:tok0 + n, :].rearrange("(b p) h -> p b h", p=128), in_=lf_sb[:, :, :]), reads=["lf_sb"])
        for bl in range(nblk):
            gb = gb0 + bl
            T.op("pe", lambda e: e.matmul(pH3[0:bs, 16:18], lhsT=cst[0:bs, C_TRI:C_TRI + bs], rhs=lf_sb[0:bs, bl, :], start=True, stop=True),
                 reads=["cst", "lf_sb"], writes=["pH3c"])
            T.op("pe", lambda e: e.matmul(pH3[:, 18:20], lhsT=cst[0:bs, C_ONES:C_ONES + 128], rhs=lf_sb[0:bs, bl, :], start=True, stop=True),
                 reads=["cst", "lf_sb"], writes=["pH3c"])
            if tt > 0 and bl == 2:
                T.op("dve", lambda e: e.tensor_copy(out=rq[:, tt, :], in_=ctot[:]), reads=["ctot"], writes=["rq"])
            T.op("dve", lambda e: e.scalar_tensor_tensor(out=negc[0:bs, gb, :], in0=pH3[0:bs, 16:18], scalar=-1.0, in1=ctot[0:bs, :],
                                                         op0=ALU.mult, op1=ALU.subtract), reads=["pH3c", "ctot"], writes=["negc"])
            T.op("dve", lambda e: e.tensor_tensor(out=ctot[:], in0=ctot[:], in1=pH3[:, 18:20], op=ALU.add), reads=["pH3c", "ctot"], writes=["ctot"])
        T.op("dve", lambda e: e.tensor_tensor(out=gg[0:bs, 0:nblk, :], in0=sig[0:bs, 0:nblk, :], in1=oml_bc[0:bs, :].unsqueeze(1).to_broadcast([bs, nblk, 128]), op=ALU.mult),
             reads=["sig", "oml_bc"], writes=["gg"])
        T.op("dve", lambda e: e.tensor_tensor(out=gg[0:bs, 0:nblk, :], in0=gg[0:bs, 0:nblk, :], in1=lb_bc[0:bs, :].unsqueeze(1).to_broadcast([bs, nblk, 128]), op=ALU.add),
             reads=["gg", "lb_bc"], writes=["gg"])
        T.op("act", lambda e: e.activation(out=logg[0:bs, 0:nblk, :], in_=gg[0:bs, 0:nblk, :], func=AF.Ln), reads=["gg"], writes=["logg"])
        T.op("dve", lambda e: e.tensor_scalar(out=kb[0:bs, 0:nblk, :], in0=gg[0:bs, 0:nblk, :], scalar1=-1.0, scalar2=1.0, op0=ALU.mult, op1=ALU.add),
             reads=["gg"], writes=["kb"])
        if tt == 0:
            ctq, cta, ctm = C_TQ16, C_TAUX16, C_TM16
        else:
            ctq, cta, ctm = C_TQ, C_TAUX, C_TM
        for bl in range(nblk):
            T.op("pe", lambda e: e.matmul(pH1[:, bl, 0:bs], lhsT=logg[0:bs, bl, :], rhs=cst[0:bs, ctq:ctq + bs], start=True, stop=True),
                 reads=["logg", "cst"], writes=["pH1"])
            T.op("pe", lambda e: e.matmul(pH3[:, 3 * bl:3 * bl + 3], lhsT=logg[0:bs, bl, :], rhs=cst[0:bs, cta:cta + 3], start=True, stop=True),
                 reads=["logg", "cst"], writes=["pH3a"])
            T.op("pe", lambda e: e.matmul(pH2[0:bs, bl, :], lhsT=cst[0:bs, ctm:ctm + bs], rhs=logg[0:bs, bl, :], start=True, stop=True),
                 reads=["logg", "cst"], writes=["pH2"])
        T.op("act", lambda e: e.activation(out=EQ[:, 0:nblk, 0:bs], in_=pH1[:, 0:nblk, 0:bs], func=AF.Exp), reads=["pH1"], writes=["EQ"])
        T.op("act", lambda e: e.activation(out=e3[:, 0:nblk, :], in_=pH3[:, 0:3 * nblk].rearrange("p (b c) -> p b c", c=3), func=AF.Exp), reads=["pH3a"], writes=["e3"])
        T.op("act", lambda e: e.activation(out=EK[0:bs, 0:nblk, :], in_=pH2[0:bs, 0:nblk, :], func=AF.Exp), reads=["pH2"], writes=["EK"])
        T.op("dve", lambda e: e.tensor_tensor(out=Q2T[:, 0:nblk, 0:bs], in0=qbT[:, 0:n].rearrange("p (b t) -> p b t", t=bs), in1=EQ[:, 0:nblk, 0:bs], op=ALU.mult),
             reads=["qbT", "EQ"], writes=["Q2T"])
        T.op("dve", lambda e: e.tensor_tensor(out=K2[0:bs, 0:nblk, :], in0=kb[0:bs, 0:nblk, :], in1=EK[0:bs, 0:nblk, :], op=ALU.mult),
             reads=["kb", "EK"], writes=["K2"])
        for bl in range(nblk):
            T.op("pe", lambda e: e.transpose(pK2T[:, bl, 0:bs], K2[0:bs, bl, :], identb[0:bs, 0:bs]), reads=["K2", "identb"], writes=["pT"])
        T.op("act", lambda e: e.activation(out=K2T[:, 0:nblk, 0:bs], in_=pK2T[:, 0:nblk, 0:bs], func=AF.Copy), reads=["pT"], writes=["K2T"])
        for bl in range(nblk):
            T.op("pe", lambda e: e.matmul(pH2[0:bs, bl, 0:bs], lhsT=K2T[:, bl, 0:bs], rhs=Q2T[:, bl, 0:bs], start=True, stop=True),
                 reads=["K2T", "Q2T"], writes=["pH2"])
        T.op("dve", lambda e: e.tensor_tensor(out=ATm[0:bs, 0:nblk, 0:bs], in0=pH2[0:bs, 0:nblk, 0:bs],
                                              in1=cst[0:bs, C_TRI:C_TRI + bs].unsqueeze(1).to_broadcast([bs, nblk, bs]), op=ALU.mult),
             reads=["pH2", "cst"], writes=["ATm"])
        for bl in range(nblk):
            T.op("dve", lambda e: e.tensor_scalar(out=S0b[:], in0=S0[:], scalar1=e3[:, bl, 0:1], scalar2=None, op0=ALU.mult), reads=["S0", "e3"], writes=["S0b"])
            T.op("pe", lambda e: e.matmul(pH1[:, bl, 0:bs], lhsT=vbb[0:bs, bl, :], rhs=ATm[0:bs, bl, 0:bs], start=True, stop=False),
                 reads=["vbb", "ATm"], writes=["pH1"])
            T.op("pe", lambda e: e.matmul(pH1[:, bl, 0:bs], lhsT=S0b[:], rhs=Q2T[:, bl, 0:bs], start=False, stop=True),
                 reads=["S0b", "Q2T"], writes=["pH1"])
            T.op("pe", lambda e: e.matmul(pH3[:, 128:256], lhsT=K2[0:bs, bl, :], rhs=vbb[0:bs, bl, :], start=True, stop=True),
                 reads=["K2", "vbb"], writes=["pH3s"])
            T.op("dve", lambda e: e.tensor_scalar(out=S0t[:], in0=S0[:], scalar1=e3[:, bl, 1:2], scalar2=None, op0=ALU.mult), reads=["S0", "e3"], writes=["S0t"])
            T.op("dve", lambda e: e.scalar_tensor_tensor(out=S0[:], in0=pH3[:, 128:256], scalar=e3[:, bl, 2:3], in1=S0t[:], op0=ALU.mult, op1=ALU.add),
                 reads=["pH3s", "e3", "S0t"], writes=["S0"])
        if tt > 0:
            pH1f = pH1[:].rearrange("p b t -> p (b t)")
            T.op("act", lambda e: e.activation(out=sq[:], in_=pH1f, func=AF.Square), reads=["pH1"], writes=["sq"])
            T.op("dve", lambda e: e.tensor_copy(out=oT[:], in_=pH1f), reads=["pH1"], writes=["oT"])
            pSS = pH2[:].rearrange("p b t -> p (b t)")
            T.op("pe", lambda e: e.matmul(pSS, lhsT=onesb[:], rhs=sq[:], start=True, stop=True), reads=["onesb", "sq"], writes=["pH2"])
            T.op("act", lambda e: e.activation(out=rstd[:], in_=pSS, func=AF.Ln, bias=EPS, scale=1.0 / 128), reads=["pH2"], writes=["rstd"])
            T.op("act", lambda e: e.activation(out=rstd[:], in_=rstd[:], func=AF.Exp, scale=-0.5), reads=["rstd"], writes=["rstd"])
            T.op("dve", lambda e: e.tensor_tensor(out=oT[:], in0=oT[:], in1=rstd[:], op=ALU.mult), reads=["oT", "rstd"], writes=["oT"])
            T.op("dve", lambda e: e.scalar_tensor_tensor(out=mixB[:], in0=oT[:], scalar=gon_sb[:, 0:1], in1=zbT[:], op0=ALU.mult, op1=ALU.mult),
                 reads=["oT", "gon_sb", "zbT"], writes=["mixB"])
            c0 = tok0 - NMETA
            T.dma("sp", lambda e: e.dma_start(out=mix_src_ap(1, c0), in_=mixB[:]), reads=["mixB"], writes=["mix_src"])

    def early_exit(*stacks):
        for st in stacks:
            st.close()
        pa.close()
        T.finish("sp")
        T.close()
        es.close()
        return nc

    for tt in range(NT + 1):
        tile_phase1(tt)
        if os.environ.get("K_STOP") == "t%d" % tt:
            return early_exit(p1)
    T.dma("sp", lambda e: e.dma_start(out=pS[:, :], in_=S0[:]), reads=["S0"])
    p1.close()
    if os.environ.get("K_STOP") == "p1":
        return early_exit()
    T.barrier()

    p2 = contextlib.ExitStack()
    NPB = 4
    Pb = [[sb(p2, "P%d_%d" % (h, i), [128, 512], BF16) for i in range(NPB)] for h in range(2)]
    bias = sb(p2, "bias", [128, 2, NB + 1])
    Osb = sb(p2, "Osb", [128, 512])
    rsum = sb(p2, "rsum", [128, 512])
    mixA = sb(p2, "mixA", [128, 512], BF16)
    pS_ = [[ps(p2, "pS%d_%d" % (h, i), [128, 512]) for i in range(2)] for h in range(2)]
    pO = [ps(p2, "pO%d" % h, [128, 512]) for h in range(2)]
    pRB = ps(p2, "pRB", [128, 512])
    cnt = [0]
    T.op("dve", lambda e: e.memset(rsum[:], 0.0), writes=["rsum"])
    for Q in range(1, NT + 1):
        q0 = NMETA + 512 * (Q - 1)
        nfull = 1 + 4 * (Q - 1)
        nblocks = nfull + 4
        for h in range(2):
            T.op("dve", lambda e: e.tensor_scalar(out=bias[:, h, 0:nblocks], in0=negc[:, 0:nblocks, h], scalar1=rq[:, Q, h:h + 1], scalar2=None, op0=ALU.add),
                 reads=["negc", "rq"], writes=["bias"])

        def blk_geom(j):
            if j == 0:
                return 0, NMETA, 0
            k0 = NMETA + 128 * (j - 1)
            jl = j - nfull
            return k0, 128, (128 * jl if jl > 0 else 0)

        def emit_qk(j):
            k0, bs, qc = blk_geom(j)
            i = cnt[0] + j
            for h in range(2):
                pt_ = pS_[h][i % 2]
                T.op("pe", lambda e: e.matmul(pt_[0:bs, qc:512], lhsT=KT[64 * h:64 * h + 64, k0:k0 + bs], rhs=QT[64 * h:64 * h + 64, q0 + qc:q0 + 512],
                                              start=True, stop=True), reads=["KT", "QT"], writes=["pS%d_%d" % (h, i % 2)])

        def emit_exp_pv(j):
            k0, bs, qc = blk_geom(j)
            i = cnt[0] + j
            jl = j - nfull
            for h in range(2):
                pt_ = pS_[h][i % 2]
                pb = Pb[h][i % NPB]
                pbn = "P%d_%d" % (h, i % NPB)
                T.op("act", lambda e: e.activation(out=pb[0:bs, qc:512], in_=pt_[0:bs, qc:512], func=AF.Exp, bias=bias[0:bs, h, j:j + 1], scale=SCALE),
                     reads=["pS%d_%d" % (h, i % 2), "bias"], writes=[pbn])
                if jl >= 0:
                    T.op("pool", lambda e: e.tensor_tensor(out=pb[:, qc:qc + 128], in0=pb[:, qc:qc + 128], in1=tri, op=ALU.mult), reads=[pbn, "cst"], writes=[pbn])
            for h in range(2):
                pb = Pb[h][i % NPB]
                pbn = "P%d_%d" % (h, i % NPB)
                T.op("pe", lambda e: e.matmul(pO[h][:, qc:512], lhsT=Vaug[0:bs, j, h, :], rhs=pb[0:bs, qc:512], start=(j == 0), stop=(j == nblocks - 1)),
                     reads=["Vaug", pbn], writes=["pO%d" % h])

        emit_qk(0)
        for j in range(nblocks):
            if j + 1 < nblocks:
                emit_qk(j + 1)
            emit_exp_pv(j)
        cnt[0] += nblocks
        T.op("dve", lambda e: e.reciprocal(out=rsum[64:65, :], in_=pO[0][64:65, :]), reads=["pO0"], writes=["rsum"])
        T.op("dve", lambda e: e.reciprocal(out=rsum[0:1, :], in_=pO[1][0:1, :]), reads=["pO1"], writes=["rsum"])
        T.op("pe", lambda e: e.matmul(pRB[:], lhsT=cst[:, C_SEL:C_SEL + 128], rhs=rsum[:], start=True, stop=True), reads=["cst", "rsum"], writes=["pRB"])
        T.op("act", lambda e: e.activation(out=Osb[0:64, :], in_=pO[0][0:64, :], func=AF.Copy), reads=["pO0"], writes=["Osb"])
        T.op("act", lambda e: e.activation(out=Osb[64:128, :], in_=pO[1][64:128, :], func=AF.Copy), reads=["pO1"], writes=["Osb"])
        T.op("dve", lambda e: e.tensor_tensor(out=Osb[:], in0=Osb[:], in1=pRB[:], op=ALU.mult), reads=["Osb", "pRB"], writes=["Osb"])
        c0 = 512 * (Q - 1)
        T.op("dve", lambda e: e.tensor_tensor(out=mixA[:], in0=Osb[:], in1=zaT[:, c0:c0 + 512], op=ALU.mult), reads=["Osb", "zaT"], writes=["mixA"])
        T.dma("sp", lambda e: e.dma_start(out=mix_src_ap(0, c0), in_=mixA[:]), reads=["mixA"], writes=["mix_src"])
        if (c0 + 512) % CH == 0:
            ci = c0 // CH
            T.cc(lambda e: e.collective_compute("AllGather", ALU.bypass, replica_groups=[[0, 1, 2, 3], [4, 5, 6, 7]],
                                                ins=[mix_src[ci * 256:(ci + 1) * 256, :]], outs=[mix_dst[ci * 1024:(ci + 1) * 1024, :]]),
                 reads=["mix_src"], writes=["mix_dst"])
    p2.close()
    if os.environ.get("K_STOP") == "p2":
        return early_exit()
    pa.close()
    T.barrier()

    p3 = contextlib.ExitStack()
    wob = sb(p3, "wob", [128, 8, D], BF16)
    gpost_bc = sb(p3, "gpost_bc", [128, D])
    qidx_sb = sb(p3, "qidx_sb", [128, 8], I32)
    mixT = sb(p3, "mixT", [128, 8, TQ], BF16)
    xres = [sb(p3, "xres%d" % i, [128, D]) for i in range(2)]
    ybuf = [sb(p3, "ybuf%d" % i, [128, D]) for i in range(2)]
    junk3 = sb(p3, "junk3", [128, 512], BF16)
    ss3 = sb(p3, "ss3", [128, 8])
    pY = [[ps(p3, "pY%d_%d" % (i, hh), [128, 512]) for hh in range(2)] for i in range(2)]
    T.dma("pool", lambda e: e.dma_start(out=wob[:], in_=wout.rearrange("(c p) n -> p c n", p=128)), writes=["wob"])
    T.dma("sp", lambda e: e.dma_start(out=gpost_bc[:], in_=gpost.partition_broadcast(128)), writes=["gpost_bc"])
    T.dma("sp", lambda e: e.dma_start(out=qidx_sb[:], in_=qidx[:, :]), writes=["qidx_sb"])
    mix_rows = mix_dst.ap().rearrange("r (q t) -> (r q) t", t=TQ)
    for kc in range(8):
        T.dma("pool", lambda e: e.indirect_dma_start(out=mixT[:, kc, :], out_offset=None, in_=mix_rows,
                                                     in_offset=bass.IndirectOffsetOnAxis(ap=qidx_sb[:, kc:kc + 1], axis=0)),
              reads=["qidx_sb", "mix_dst"], writes=["mixT"])
    for tb in range(TQ // 128):
        k = tb % 2
        xr, yb = xres[k], ybuf[k]
        T.dma("sp", lambda e: e.dma_start(out=xr[:], in_=xq[tb * 128:(tb + 1) * 128, :]), writes=["xres%d" % k])
        for hh in range(2):
            for kc in range(8):
                wrow = (kc // 2) + 4 * (kc % 2)
                T.op("pe", lambda e: e.matmul(pY[k][hh][:], lhsT=mixT[:, kc, tb * 128:(tb + 1) * 128], rhs=wob[:, wrow, hh * 512:(hh + 1) * 512],
                                              start=(kc == 0), stop=(kc == 7)), reads=["mixT", "wob"], writes=["pY%d_%d" % (k, hh)])
        for hh in range(2):
            T.op("act", lambda e: e.activation(out=junk3[:], in_=pY[k][hh][:], func=AF.Square, accum_out=ss3[:, hh:hh + 1]),
                 reads=["pY%d_%d" % (k, hh)], writes=["junk3", "ss3"])
        T.op("dve", lambda e: e.tensor_tensor(out=ss3[:, 2:3], in0=ss3[:, 0:1], in1=ss3[:, 1:2], op=ALU.add), reads=["ss3"], writes=["ss3"])
        T.op("act", lambda e: e.activation(out=ss3[:, 3:4], in_=ss3[:, 2:3], func=AF.Ln, bias=EPS, scale=1.0 / D), reads=["ss3"], writes=["ss3"])
        T.op("act", lambda e: e.activation(out=ss3[:, 4:5], in_=ss3[:, 3:4], func=AF.Exp, scale=-0.5), reads=["ss3"], writes=["ss3"])
        for hh in range(2):
            T.op("dve", lambda e: e.scalar_tensor_tensor(out=yb[:, hh * 512:(hh + 1) * 512], in0=pY[k][hh][:], scalar=ss3[:, 4:5],
                                                         in1=gpost_bc[:, hh * 512:(hh + 1) * 512], op0=ALU.mult, op1=ALU.mult),
                 reads=["pY%d_%d" % (k, hh), "ss3", "gpost_bc"], writes=["ybuf%d" % k])
        T.op("pool", lambda e: e.tensor_tensor(out=yb[:], in0=yb[:], in1=xr[:], op=ALU.add), reads=["ybuf%d" % k, "xres%d" % k], writes=["ybuf%d" % k])
        T.dma("sp", lambda e: e.dma_start(out=y_q[tb * 128:(tb + 1) * 128, :], in_=yb[:]), reads=["ybuf%d" % k])
    p3.close()
    T.barrier()

    SBI = min(4, cfg.SB)
    NGRP = cfg.SB // SBI
    NPG = cfg.NPG
    NG = NPG // 4
    IN_W = 4104
    O_QA, O_KA, O_VA, O_FA, O_ZA, O_QB, O_FB, O_VB, O_ZB = 0, 512, 1024, 1536, 1544, 2056, 2568, 3080, 3592
    if do_sample:
        xs_all = din("xs", [cfg.NS, D])
        state_all = din("state", [cfg.SB, 4, 128, 128])
        ptab_all = din("ptab", [1, cfg.SB * NPG], I32)
        ck_d = din("ck", [cfg.NPOOL * 32, 2048])
        cv_d = din("cv", [cfg.NPOOL * 32, 2048])
        clf_d = din("clf", [cfg.NPOOL * 32, 32])
        win_d = din("win", [D, IN_W])
        bf8_d = din("bf8", [1, 8])
        hlrows_d = din("hlrows", [1, 1024])
        hlT_d = din("hlT", [128, 8])
        gonrow_d = din("gonrow", [1, 128])
        cst2_d = din("cst2", [128, c2["END"]])
        ys_all = dout("ys", [cfg.NS, D])
        sk_all = dout("sk", [cfg.NS, 512])
        sv_all = dout("sv", [cfg.NS, 512])
        slf_all = dout("slf", [cfg.NS, 8])
        sS_all = dout("sS", [cfg.SB, 4, 128, 128])

    def sample_group(g):
        SB, NS = SBI, 4 * SBI
        sfx[0] = "_g%d" % g
        xs_d = xs_all[NS * g:NS * (g + 1), :]
        state_d = state_all[SB * g:SB * (g + 1)]
        ptab_d = ptab_all[:, SB * NPG * g:SB * NPG * (g + 1)]
        ys_o = ys_all[NS * g:NS * (g + 1), :]
        sk_o = sk_all[NS * g:NS * (g + 1), :]
        sv_o = sv_all[NS * g:NS * (g + 1), :]
        slf_o = slf_all[NS * g:NS * (g + 1), :]
        sS_o = sS_all[SB * g:SB * (g + 1)]

        sp_ = contextlib.ExitStack()
        cst2 = sb(sp_, "cst2", [128, c2["END"]])
        T.dma("sp", lambda e: e.dma_start(out=cst2[:], in_=cst2_d[:, :]), writes=["cst2"])
        BT = cst2[0:NS, c2["BT"]:c2["BT"] + NS]
        BU = cst2[0:NS, c2["BU"]:c2["BU"] + NS]
        BT128 = cst2[:, c2["BT"]:c2["BT"] + 128]
        BU128 = cst2[:, c2["BU"]:c2["BU"] + 128]
        MNEW = cst2[0:NS, c2["MNEW"]:c2["MNEW"] + SB * 32]
        BMT = cst2[:, c2["BMT"]:c2["BMT"] + SB * NS]
        BM = cst2[0:NS, c2["BM"]:c2["BM"] + SB]
        SUP = cst2[:, c2["SUP"]:c2["SUP"] + 128]
        SELP = cst2[:, c2["SELP"]:c2["SELP"] + 4]
        GCOL = cst2[:, c2["GCOL"]:c2["GCOL"] + 1]
        ONESF = cst[:, C_ONES:C_ONES + 128]

        u = sb(sp_, "u", [NS, IN_W])
        QaT = sb(sp_, "QaT", [128, 4, NS], BF16)
        KnT = sb(sp_, "KnT", [128, 4, NS], BF16)
        Qbd = sb(sp_, "Qbd", [128, 4, SB, 8], BF16)
        Q1pad = sb(sp_, "Q1pad", [128, 4, SB, NS], BF16)
        K3pad = sb(sp_, "K3pad", [NS, SB, 512], BF16)
        ATs = sb(sp_, "ATs", [NS, 4, NS], BF16)
        EbT = sb(sp_, "EbT", [128, 4, NS])
        vbs = sb(sp_, "vbs", [NS, 512], BF16)
        vas = sb(sp_, "vas", [NS, 512], BF16)
        Pnew = sb(sp_, "Pnew", [NS, SB, 32], BF16)
        idx_all = sb(sp_, "idx_all", [128, SB * NG], I32)
        mixs = sb(sp_, "mixs", [NS, D], BF16)
        oa_all = sb(sp_, "oa_all", [NS, 512])
        xs_sb = sb(sp_, "xs_sb", [NS, D])
        gpost_s = sb(sp_, "gpost_s", [NS, D])
        wob_s = sb(sp_, "wob_s", [128, 8, D], BF16)
        T.dma("sp", lambda e: e.dma_start(out=xs_sb[:], in_=xs_d[:, :]), writes=["xs_sb"])
        T.dma("sp", lambda e: e.dma_start(out=gpost_s[:], in_=gpost.partition_broadcast(NS)), writes=["gpost_s"])

        sa = contextlib.ExitStack()
        pt_i = sb(sa, "pt_i", [128, SB * NPG], I32)
        pt_f = sb(sa, "pt_f", [128, SB * NG, 4])
        idx_f = sb(sa, "idx_f", [128, SB * NG])
        T.dma("sp", lambda e: e.dma_start(out=pt_i[:], in_=ptab_d.partition_broadcast(128)), writes=["pt_i"])
        T.op("dve", lambda e: e.tensor_copy(out=pt_f[:].rearrange("p a b -> p (a b)"), in_=pt_i[:]), reads=["pt_i"], writes=["pt_f"])
        T.op("dve", lambda e: e.tensor_tensor(out=pt_f[:], in0=pt_f[:], in1=SELP.unsqueeze(1).to_broadcast([128, SB * NG, 4]), op=ALU.mult),
             reads=["pt_f", "cst2"], writes=["pt_f"])
        T.op("dve", lambda e: e.tensor_reduce(out=idx_f[:], in_=pt_f[:], axis=AX.X, op=ALU.add), reads=["pt_f"], writes=["idx_f"])
        T.op("dve", lambda e: e.tensor_scalar(out=idx_f[:], in0=idx_f[:], scalar1=32.0, scalar2=GCOL, op0=ALU.mult, op1=ALU.add),
             reads=["idx_f", "cst2"], writes=["idx_f"])
        T.op("dve", lambda e: e.tensor_copy(out=idx_all[:], in_=idx_f[:]), reads=["idx_f"], writes=["idx_all"])

        wF = sb(sa, "wF", [128, 8, 2048], BF16)
        wTt = [sb(sa, "wTt%d" % i, [128, 8, 512], BF16) for i in range(2)]
        wst = [sb(sa, "wst%d" % i, [128, 8, 512]) for i in range(2)]
        wst_n = [0]

        def load_cast(dst, dst_name, src, w):
            i = wst_n[0] % 2
            wst_n[0] += 1
            st = wst[i]
            T.dma("sp", lambda e: e.dma_start(out=st[:, :, 0:w], in_=src.rearrange("(c p) n -> p c n", p=128)), writes=["wst%d" % i])
            T.op("act", lambda e: e.activation(out=dst, in_=st[:, :, 0:w], func=AF.Copy), reads=["wst%d" % i], writes=[dst_name])
        gpre_s = sb(sa, "gpre_s", [NS, D])
        hns = sb(sa, "hns", [NS, D], BF16)
        hnTs = sb(sa, "hnTs", [128, 8, NS], BF16)
        junks = sb(sa, "junks", [NS, D], BF16)
        sss = sb(sa, "sss", [NS, 8])
        for r, off in enumerate([O_QA, O_KA, O_QB, O_FB]):
            load_cast(wF[:, :, r * 512:(r + 1) * 512], "wF", win_d[:, off:off + 512], 512)
        T.dma("sp", lambda e: e.dma_start(out=gpre_s[:], in_=gpre.partition_broadcast(NS)), writes=["gpre_s"])
        T.op("act", lambda e: e.activation(out=junks[:], in_=xs_sb[:], func=AF.Square, accum_out=sss[:, 0:1]), reads=["xs_sb"], writes=["junks", "sss"])
        T.op("act", lambda e: e.activation(out=sss[:, 1:2], in_=sss[:, 0:1], func=AF.Ln, bias=EPS, scale=1.0 / D), reads=["sss"], writes=["sss"])
        T.op("act", lambda e: e.activation(out=sss[:, 2:3], in_=sss[:, 1:2], func=AF.Exp, scale=-0.5), reads=["sss"], writes=["sss"])
        T.op("dve", lambda e: e.scalar_tensor_tensor(out=hns[:], in0=xs_sb[:], scalar=sss[:, 2:3], in1=gpre_s[:], op0=ALU.mult, op1=ALU.mult),
             reads=["xs_sb", "sss", "gpre_s"], writes=["hns"])
        psA = [ps(sa, "psA%d" % i, [128, 512]) for i in range(2)]
        psB = [ps(sa, "psB%d" % i, [128, 8, NS]) for i in range(2)]
        psC = ps(sa, "psC", [128, 512])
        pTs = ps(sa, "pTs", [128, 8, 128], BF16)
        for kc in range(8):
            T.op("pe", lambda e: e.transpose(pTs[:, kc, 0:NS], hns[:, kc * 128:(kc + 1) * 128], identb[0:NS, 0:NS]), reads=["hns", "identb"], writes=["pTs"])
        T.op("act", lambda e: e.activation(out=hnTs[:], in_=pTs[:, :, 0:NS], func=AF.Copy), reads=["pTs"], writes=["hnTs"])
        ncg = (IN_W + 511) // 512
        for cg in range(ncg):
            w = min(512, IN_W - cg * 512)
            pa_ = psA[cg % 2]
            nm = "psA%d" % (cg % 2)
            wt = wTt[cg % 2]
            wtn = "wTt%d" % (cg % 2)
            load_cast(wt[:, :, 0:w], wtn, win_d[:, cg * 512:cg * 512 + w], w)
            for kc in range(8):
                T.op("pe", lambda e: e.matmul(pa_[0:NS, 0:w], lhsT=hnTs[:, kc, :], rhs=wt[:, kc, 0:w], start=(kc == 0), stop=(kc == 7)),
                     reads=["hnTs", wtn], writes=[nm])
            if cg % 2 == 0:
                T.op("act", lambda e: e.activation(out=u[:, cg * 512:cg * 512 + w], in_=pa_[0:NS, 0:w], func=AF.Copy), reads=[nm], writes=["u"])
            else:
                T.op("dve", lambda e: e.tensor_copy(out=u[:, cg * 512:cg * 512 + w], in_=pa_[0:NS, 0:w]), reads=[nm], writes=["u"])
        T.dma("sp", lambda e: e.dma_start(out=sk_o[:, :], in_=u[:, O_KA:O_KA + 512]), reads=["u"])
        T.dma("sp", lambda e: e.dma_start(out=sv_o[:, :], in_=u[:, O_VA:O_VA + 512]), reads=["u"])
        for g2 in range(2):
            for ci in range(8):
                c0 = g2 * 1024 + ci * 128
                for kc in range(8):
                    T.op("pe", lambda e: e.matmul(psB[g2][:, ci, :], lhsT=wF[:, kc, c0:c0 + 128], rhs=hnTs[:, kc, :], start=(kc == 0), stop=(kc == 7)),
                         reads=["wF", "hnTs"], writes=["psB%d" % g2])
        T.op("act", lambda e: e.activation(out=QaT[:], in_=psB[0][:, 0:4, :], func=AF.Copy, scale=SCALE), reads=["psB0"], writes=["QaT"])
        T.op("act", lambda e: e.activation(out=KnT[:], in_=psB[0][:, 4:8, :], func=AF.Copy), reads=["psB0"], writes=["KnT"])
        T.op("pool", lambda e: e.memset(Qbd[:], 0.0), writes=["Qbd"])
        for half in range(2):
            T.op("dve", lambda e: e.tensor_copy(out=Qbd[64 * half:64 * half + 64, :, :, 4 * half:4 * half + 4],
                                                in_=QaT[64 * half:64 * half + 64, :, :].rearrange("p c (b q) -> p c b q", q=4)),
                 reads=["QaT"], writes=["Qbd"])
        qbTs = sb(sa, "qbTs", [128, 4, NS])
        gT = sb(sa, "gT", [128, 4, NS])
        lbT = sb(sa, "lbT", [128, 4])
        omlT = sb(sa, "omlT", [128, 4])
        hlT = sb(sa, "hlT", [128, 4, 2])
        hl2 = sb(sa, "hl2", [NS, 2, 512])
        lb_s = sb(sa, "lb_s", [NS, 512])
        oml_s = sb(sa, "oml_s", [NS, 512])
        gs = sb(sa, "gs", [NS, 512])
        loggs = sb(sa, "loggs", [128, 512])
        kbs = sb(sa, "kbs", [NS, 512])
        E3s = sb(sa, "E3s", [NS, 512])
        K3s = sb(sa, "K3s", [NS, 512], BF16)
        EnbT = sb(sa, "EnbT", [128, 4, NS])
        Q1T = sb(sa, "Q1T", [128, 4, NS], BF16)
        K2Ts = sb(sa, "K2Ts", [128, 4, NS], BF16)
        T.dma("sp", lambda e: e.dma_start(out=hlT[:].rearrange("p a b -> p (a b)"), in_=hlT_d[:, :]), writes=["hlT"])
        T.dma("sp", lambda e: e.dma_start(out=hl2[:].rearrange("p a b -> p (a b)"), in_=hlrows_d.partition_broadcast(NS)), writes=["hl2"])
        T.op("dve", lambda e: e.tensor_tensor(out=lbT[:], in0=hlT[:, :, 0], in1=hlT[:, :, 1], op=ALU.subtract), reads=["hlT"], writes=["lbT"])
        T.op("act", lambda e: e.activation(out=lbT[:], in_=lbT[:], func=AF.Sigmoid), reads=["lbT"], writes=["lbT"])
        T.op("dve", lambda e: e.tensor_scalar(out=omlT[:], in0=lbT[:], scalar1=-1.0, scalar2=1.0, op0=ALU.mult, op1=ALU.add), reads=["lbT"], writes=["omlT"])
        T.op("dve", lambda e: e.tensor_tensor(out=lb_s[:], in0=hl2[:, 0, :], in1=hl2[:, 1, :], op=ALU.subtract), reads=["hl2"], writes=["lb_s"])
        T.op("act", lambda e: e.activation(out=lb_s[:], in_=lb_s[:], func=AF.Sigmoid), reads=["lb_s"], writes=["lb_s"])
        T.op("dve", lambda e: e.tensor_scalar(out=oml_s[:], in0=lb_s[:], scalar1=-1.0, scalar2=1.0, op0=ALU.mult, op1=ALU.add), reads=["lb_s"], writes=["oml_s"])
        T.op("act", lambda e: e.activation(out=qbTs[:], in_=psB[1][:, 0:4, :], func=AF.Silu), reads=["psB1"], writes=["qbTs"])
        T.op("act", lambda e: e.activation(out=gT[:], in_=psB[1][:, 4:8, :], func=AF.Sigmoid), reads=["psB1"], writes=["gT"])
        T.op("dve", lambda e: e.tensor_tensor(out=gT[:], in0=gT[:], in1=omlT[:].unsqueeze(2).to_broadcast([128, 4, NS]), op=ALU.mult), reads=["gT", "omlT"], writes=["gT"])
        T.op("dve", lambda e: e.tensor_tensor(out=gT[:], in0=gT[:], in1=lbT[:].unsqueeze(2).to_broadcast([128, 4, NS]), op=ALU.add), reads=["gT", "lbT"], writes=["gT"])
        T.op("act", lambda e: e.activation(out=gs[:], in_=u[:, O_FB:O_FB + 512], func=AF.Sigmoid), reads=["u"], writes=["gs"])
        T.op("dve", lambda e: e.tensor_tensor(out=gs[:], in0=gs[:], in1=oml_s[:], op=ALU.mult), reads=["gs", "oml_s"], writes=["gs"])
        T.op("dve", lambda e: e.tensor_tensor(out=gs[:], in0=gs[:], in1=lb_s[:], op=ALU.add), reads=["gs", "lb_s"], writes=["gs"])
        T.op("pool", lambda e: e.memset(loggs[:], 0.0), writes=["loggs"])
        T.op("act", lambda e: e.activation(out=loggs[0:NS, :], in_=gs[:], func=AF.Ln), reads=["gs"], writes=["loggs"])
        T.op("dve", lambda e: e.tensor_scalar(out=kbs[:], in0=gs[:], scalar1=-1.0, scalar2=1.0, op0=ALU.mult, op1=ALU.add), reads=["gs"], writes=["kbs"])
        for h in range(4):
            T.op("pe", lambda e: e.matmul(psB[0][:, h, :], lhsT=loggs[:, h * 128:(h + 1) * 128], rhs=BT128[:, 0:NS], start=True, stop=True),
                 reads=["loggs", "cst2"], writes=["psB0"])
        T.op("pe", lambda e: e.matmul(psC[:, :], lhsT=BU128, rhs=loggs[:], start=True, stop=True), reads=["loggs", "cst2"], writes=["psC"])
        T.op("act", lambda e: e.activation(out=EbT[:], in_=psB[0][:, 0:4, :], func=AF.Exp), reads=["psB0"], writes=["EbT"])
        T.op("act", lambda e: e.activation(out=EnbT[:], in_=psB[0][:, 0:4, :], func=AF.Exp, scale=-1.0), reads=["psB0"], writes=["EnbT"])
        T.op("act", lambda e: e.activation(out=E3s[:], in_=psC[0:NS, :], func=AF.Exp), reads=["psC"], writes=["E3s"])
        T.op("dve", lambda e: e.tensor_tensor(out=Q1T[:], in0=qbTs[:], in1=EbT[:], op=ALU.mult), reads=["qbTs", "EbT"], writes=["Q1T"])
        T.op("dve", lambda e: e.tensor_scalar(out=gT[:], in0=gT[:], scalar1=-1.0, scalar2=1.0, op0=ALU.mult, op1=ALU.add), reads=["gT"], writes=["gT"])
        T.op("dve", lambda e: e.tensor_tensor(out=K2Ts[:], in0=gT[:], in1=EnbT[:], op=ALU.mult), reads=["gT", "EnbT"], writes=["K2Ts"])
        T.op("dve", lambda e: e.tensor_tensor(out=K3s[:], in0=kbs[:], in1=E3s[:], op=ALU.mult), reads=["kbs", "E3s"], writes=["K3s"])
        for b0 in range(0, SB, 4):
            nb4 = min(4, SB - b0)
            T.op("dve", lambda e: e.tensor_tensor(out=K3pad[:, b0:b0 + nb4, :], in0=K3s[:].unsqueeze(1).to_broadcast([NS, nb4, 512]),
                                                  in1=BM[:, b0:b0 + nb4].unsqueeze(2).to_broadcast([NS, nb4, 512]), op=ALU.mult), reads=["K3s", "cst2"], writes=["K3pad"])
        bmt3 = BMT.rearrange("p (b t) -> p b t", t=NS)
        for h in range(4):
            T.op("pool", lambda e: e.tensor_tensor(out=Q1pad[:, h, :, :], in0=Q1T[:, h, :].unsqueeze(1).to_broadcast([128, SB, NS]), in1=bmt3, op=ALU.mult),
                 reads=["Q1T", "cst2"], writes=["Q1pad"])
        T.op("dve", lambda e: e.tensor_copy(out=vbs[:], in_=u[:, O_VB:O_VB + 512]), reads=["u"], writes=["vbs"])
        T.op("dve", lambda e: e.tensor_copy(out=vas[:], in_=u[:, O_VA:O_VA + 512]), reads=["u"], writes=["vas"])
        for h in range(4):
            T.op("pe", lambda e: e.matmul(psB[1][0:NS, h, :], lhsT=K2Ts[:, h, :], rhs=Q1T[:, h, :], start=True, stop=True),
                 reads=["K2Ts", "Q1T"], writes=["psB1"])
        T.op("dve", lambda e: e.tensor_tensor(out=ATs[:], in0=psB[1][0:NS, 0:4, :], in1=BT.unsqueeze(1).to_broadcast([NS, 4, NS]), op=ALU.mult),
             reads=["psB1", "cst2"], writes=["ATs"])
        bf8s = sb(sa, "bf8s", [NS, 8])
        lfn = sb(sa, "lfn", [128, 8])
        ncn = sb(sa, "ncn", [NS, 8])
        snw = sb(sa, "snw", [NS, SB, 32])
        T.dma("sp", lambda e: e.dma_start(out=bf8s[:], in_=bf8_d.partition_broadcast(NS)), writes=["bf8s"])
        T.op("pool", lambda e: e.memset(lfn[:], 0.0), writes=["lfn"])
        T.op("dve", lambda e: e.tensor_tensor(out=lfn[0:NS, :], in0=u[:, O_FA:O_FA + 8], in1=bf8s[:], op=ALU.add), reads=["u", "bf8s"], writes=["lfn"])
        T.op("act", lambda e: e.activation(out=lfn[0:NS, :], in_=lfn[0:NS, :], func=AF.Exp, scale=-1.0), reads=["lfn"], writes=["lfn"])
        T.op("act", lambda e: e.activation(out=lfn[0:NS, :], in_=lfn[0:NS, :], func=AF.Ln, bias=1.0, scale=1.0), reads=["lfn"], writes=["lfn"])
        T.op("dve", lambda e: e.tensor_scalar(out=lfn[0:NS, :], in0=lfn[0:NS, :], scalar1=-1.0, scalar2=None, op0=ALU.mult), reads=["lfn"], writes=["lfn"])
        T.dma("sp", lambda e: e.dma_start(out=slf_o[:, :], in_=lfn[0:NS, :]), reads=["lfn"])
        T.op("pe", lambda e: e.matmul(psC[:, 0:8], lhsT=BT128, rhs=lfn[:], start=True, stop=True), reads=["lfn", "cst2"], writes=["psC"])
        T.op("dve", lambda e: e.tensor_scalar(out=ncn[:], in0=psC[0:NS, 0:8], scalar1=-1.0, scalar2=None, op0=ALU.mult), reads=["psC"], writes=["ncn"])
        pSn = psA[0][0:NS, :].rearrange("p (b c) -> p b c", c=32)[:, 0:SB, :]
        for hc in range(4):
            T.op("pe", lambda e: e.matmul(pSn[:, :, 8 * hc:8 * hc + 8], lhsT=KnT[:, hc, :], rhs=Qbd[:, hc, :, :], start=True, stop=True),
                 reads=["KnT", "Qbd"], writes=["psA0"])
        T.op("dve", lambda e: e.tensor_tensor(out=snw[:].rearrange("p b (h q) -> p b h q", q=4), in0=pSn.rearrange("p b (h q) -> p b h q", q=4),
                                              in1=ncn[:].unsqueeze(1).unsqueeze(3).to_broadcast([NS, SB, 8, 4]), op=ALU.add),
             reads=["psA0", "ncn"], writes=["snw"])
        T.op("act", lambda e: e.activation(out=snw[:], in_=snw[:], func=AF.Exp), reads=["snw"], writes=["snw"])
        T.op("dve", lambda e: e.tensor_tensor(out=Pnew[:], in0=snw[:], in1=MNEW.rearrange("p (b c) -> p b c", c=32), op=ALU.mult), reads=["snw", "cst2"], writes=["Pnew"])
        for hh in range(2):
            load_cast(wob_s[:, :, hh * 512:(hh + 1) * 512], "wob_s", wout[:, hh * 512:(hh + 1) * 512], 512)
        sa.close()
        T.barrier()

        sg = contextlib.ExitStack()
        Kt = [sb(sg, "Kt%d" % i, [128, NPG, 512], BF16) for i in range(2)]
        Vt = [sb(sg, "Vt%d" % i, [128, NPG, 512], BF16) for i in range(2)]
        Lt = [sb(sg, "Lt%d" % i, [128, NG, 4, 8]) for i in range(2)]
        Sf = [sb(sg, "Sf%d" % i, [128, 4, 128]) for i in range(2)]
        Sbf = [sb(sg, "Sbf%d" % i, [128, 4, 128], BF16) for i in range(2)]

        def issue_gather(bq):
            k = bq % 2
            for i in range(NG):
                col = bq * NG + i
                off = bass.IndirectOffsetOnAxis(ap=idx_all[:, col:col + 1], axis=0)
                T.dma("pool", lambda e: e.indirect_dma_start(out=Kt[k][:, 4 * i:4 * i + 4, :].rearrange("p a b -> p (a b)"), out_offset=None,
                                                             in_=ck_d[:, :], in_offset=off), reads=["idx_all"], writes=["Kt%d" % k])
                T.dma("pool", lambda e: e.indirect_dma_start(out=Vt[k][:, 4 * i:4 * i + 4, :].rearrange("p a b -> p (a b)"), out_offset=None,
                                                             in_=cv_d[:, :], in_offset=off), reads=["idx_all"], writes=["Vt%d" % k])
                T.dma("pool", lambda e: e.indirect_dma_start(out=Lt[k][:, i, :, :].rearrange("p a b -> p (a b)"), out_offset=None,
                                                             in_=clf_d[:, :], in_offset=off), reads=["idx_all"], writes=["Lt%d" % k])
            T.dma("sp", lambda e: e.dma_start(out=Sf[k][:], in_=state_d[bq].rearrange("h k v -> k h v")), writes=["Sf%d" % k])

        def cast_state(bq):
            k = bq % 2
            T.op("act", lambda e: e.activation(out=Sbf[k][:], in_=Sf[k][:], func=AF.Copy), reads=["Sf%d" % k], writes=["Sbf%d" % k])

        issue_gather(0)
        cast_state(0)

        sl = contextlib.ExitStack()
        KTs = sb(sl, "KTs", [128, NPG, 4, 128], BF16)
        Wl = sb(sl, "Wl", [128, NG, 8])
        V3 = sb(sl, "V3", [128, NG, 4, 8])
        Rl = sb(sl, "Rl", [128, NG, 4, 8])
        scb = sb(sl, "scb", [128, NPG, 32])
        Pp = sb(sl, "Pp", [128, NPG, 32], BF16)
        Psm = sb(sl, "Psm", [128, 32])
        rs_ = sb(sl, "rs_", [4, 8])
        oab = [sb(sl, "oab%d" % i, [4, 512]) for i in range(2)]
        Snew = [sb(sl, "Snew%d" % i, [128, 4, 128]) for i in range(2)]
        pTr = [ps(sl, "pTr%d" % i, [128, 8, 128], BF16) for i in range(2)]
        pSc_full = ps(sl, "pSc", [128, 16, 32])
        pSc = pSc_full[:, 0:NPG, :]
        pR1 = ps(sl, "pR1", [128, NG, 8])
        pOs = ps(sl, "pOs", [4, 8, 64])
        pSm = ps(sl, "pSm", [4, 8])
        pOh = ps(sl, "pOh", [NS, 4, 128])
        pSU = ps(sl, "pSU", [128, 4, 128])
        for h in range(4):
            T.op("pe", lambda e: e.matmul(pOh[:, h, :], lhsT=ATs[:, h, :], rhs=vbs[:, h * 128:(h + 1) * 128], start=(h == 0), stop=False, skip_group_check=True),
                 reads=["ATs", "vbs"], writes=["pOh"])
        tcount = [0]
        for bq in range(SB):
            k = bq % 2
            if bq + 1 < SB:
                issue_gather(bq + 1)
            for s2 in range(NPG // 2):
                pt2 = pTr[tcount[0] % 2]
                nm = "pTr%d" % (tcount[0] % 2)
                for sl_ in range(2):
                    slot = 2 * s2 + sl_
                    for hc in range(4):
                        T.op("pe", lambda e: e.transpose(pt2[:, 4 * sl_ + hc, :], Kt[k][:, slot, hc * 128:(hc + 1) * 128], identb[:]),
                             reads=["Kt%d" % k, "identb"], writes=[nm])
                dst = KTs[:, 2 * s2:2 * s2 + 2, :, :].rearrange("p a b c -> p (a b) c")
                T.op("act", lambda e: e.activation(out=dst, in_=pt2[:], func=AF.Copy), reads=[nm], writes=["KTs"])
                tcount[0] += 1
            for slot in range(NPG):
                for hc in range(4):
                    T.op("pe", lambda e: e.matmul(pSc[:, slot, 8 * hc:8 * hc + 8], lhsT=KTs[:, slot, hc, :], rhs=Qbd[:, hc, bq, :], start=True, stop=True),
                         reads=["KTs", "Qbd"], writes=["pSc"])
            T.op("dve", lambda e: e.tensor_reduce(out=Wl[:], in_=Lt[k][:].rearrange("p i t h -> p i h t"), axis=AX.X, op=ALU.add), reads=["Lt%d" % k], writes=["Wl"])
            first = True
            for i in range(NG):
                T.op("pe", lambda e: e.matmul(pR1[:, i, :], lhsT=SUP, rhs=Wl[:, i, :], start=first, stop=False, skip_group_check=True), reads=["cst2", "Wl"], writes=["pR1"])
                first = False
                for i2 in range(i + 1, NG):
                    T.op("pe", lambda e: e.matmul(pR1[:, i, :], lhsT=ONESF, rhs=Wl[:, i2, :], start=False, stop=False, skip_group_check=True),
                         reads=["cst", "Wl"], writes=["pR1"])
            T.op("dve", lambda e: e.memset(V3[:, :, 3, :], 0.0), writes=["V3"])
            T.op("dve", lambda e: e.tensor_copy(out=V3[:, :, 2, :], in_=Lt[k][:, :, 3, :]), reads=["Lt%d" % k], writes=["V3"])
            T.op("dve", lambda e: e.tensor_tensor(out=V3[:, :, 1, :], in0=V3[:, :, 2, :], in1=Lt[k][:, :, 2, :], op=ALU.add), reads=["Lt%d" % k, "V3"], writes=["V3"])
            T.op("dve", lambda e: e.tensor_tensor(out=V3[:, :, 0, :], in0=V3[:, :, 1, :], in1=Lt[k][:, :, 1, :], op=ALU.add), reads=["Lt%d" % k, "V3"], writes=["V3"])
            T.op("dve", lambda e: e.tensor_tensor(out=Rl[:], in0=V3[:], in1=pR1[:].unsqueeze(2).to_broadcast([128, NG, 4, 8]), op=ALU.add),
                 reads=["V3", "pR1"], writes=["Rl"])
            T.op("dve", lambda e: e.tensor_tensor(out=scb[:].rearrange("p s (h q) -> p s h q", q=4), in0=pSc.rearrange("p s (h q) -> p s h q", q=4),
                                                  in1=Rl[:].rearrange("p i t h -> p (i t) h").unsqueeze(3).to_broadcast([128, NPG, 8, 4]), op=ALU.add),
                 reads=["pSc", "Rl"], writes=["scb"])
            T.op("act", lambda e: e.activation(out=Pp[:], in_=scb[:], func=AF.Exp), reads=["scb"], writes=["Pp"])
            if bq + 1 < SB:
                cast_state(bq + 1)
            T.op("dve", lambda e: e.tensor_reduce(out=Psm[:], in_=Pp[:].rearrange("p s c -> p c s"), axis=AX.X, op=ALU.add), reads=["Pp"], writes=["Psm"])
            first = True
            for slot in range(NPG):
                for h in range(8):
                    T.op("pe", lambda e: e.matmul(pOs[:, h, :], lhsT=Pp[:, slot, 4 * h:4 * h + 4], rhs=Vt[k][:, slot, h * 64:(h + 1) * 64],
                                                  start=first, stop=False, skip_group_check=True), reads=["Pp", "Vt%d" % k], writes=["pOs"])
                    first = False
            for h in range(8):
                T.op("pe", lambda e: e.matmul(pOs[:, h, :], lhsT=Pnew[:, bq, 4 * h:4 * h + 4], rhs=vas[:, h * 64:(h + 1) * 64],
                                              start=False, stop=(h == 7), skip_group_check=True), reads=["Pnew", "vas"], writes=["pOs"])
            for h in range(8):
                T.op("pe", lambda e: e.matmul(pSm[:, h:h + 1], lhsT=Psm[:, 4 * h:4 * h + 4], rhs=ONESF[:, 0:1], start=(h == 0), stop=False, skip_group_check=True),
                     reads=["Psm", "cst"], writes=["pSm"])
            for h in range(8):
                T.op("pe", lambda e: e.matmul(pSm[:, h:h + 1], lhsT=Pnew[:, bq, 4 * h:4 * h + 4], rhs=onesb[0:NS, 0:1], start=False, stop=(h == 7), skip_group_check=True),
                     reads=["Pnew", "onesb"], writes=["pSm"])
            T.op("dve", lambda e: e.reciprocal(out=rs_[:], in_=pSm[:]), reads=["pSm"], writes=["rs_"])
            T.op("dve", lambda e: e.tensor_tensor(out=oab[k][:].rearrange("p (h d) -> p h d", d=64), in0=pOs[:], in1=rs_[:].unsqueeze(2).to_broadcast([4, 8, 64]), op=ALU.mult),
                 reads=["pOs", "rs_"], writes=["oab%d" % k])
            T.dma("sp", lambda e: e.dma_start(out=oa_all[4 * bq:4 * bq + 4, :], in_=oab[k][:]), reads=["oab%d" % k], writes=["oa_all"])
            for h in range(4):
                T.op("pe", lambda e: e.matmul(pOh[:, h, :], lhsT=Q1pad[:, h, bq, :], rhs=Sbf[k][:, h, :], start=False, stop=(bq == SB - 1 and h == 3), skip_group_check=True),
                     reads=["Q1pad", "Sbf%d" % k], writes=["pOh"])
            for h in range(4):
                T.op("pe", lambda e: e.matmul(pSU[:, h, :], lhsT=K3pad[:, bq, h * 128:(h + 1) * 128], rhs=vbs[:, h * 128:(h + 1) * 128], start=True, stop=True),
                     reads=["K3pad", "vbs"], writes=["pSU"])
            for h in range(4):
                T.op("dve", lambda e: e.scalar_tensor_tensor(out=Snew[k][:, h, :], in0=Sf[k][:, h, :], scalar=EbT[:, h, 4 * bq + 3:4 * bq + 4], in1=pSU[:, h, :],
                                                             op0=ALU.mult, op1=ALU.add), reads=["Sf%d" % k, "EbT", "pSU"], writes=["Snew%d" % k])
            T.dma("sp", lambda e: e.dma_start(out=sS_o[bq].rearrange("h k v -> k h v"), in_=Snew[k][:]), reads=["Snew%d" % k])

        so = contextlib.ExitStack()
        zsil = sb(so, "zsil", [NS, 1024])
        ssh = sb(so, "ssh", [NS, 12])
        junk4 = sb(so, "junk4", [NS, 512], BF16)
        ob_s = sb(so, "ob_s", [NS, 4, 128])
        gon_s = sb(so, "gon_s", [NS, 128])
        mixTs = sb(so, "mixTs", [128, 8, NS], BF16)
        ysb = sb(so, "ysb", [NS, D])
        T.dma("sp", lambda e: e.dma_start(out=gon_s[:], in_=gonrow_d.partition_broadcast(NS)), writes=["gon_s"])
        T.op("act", lambda e: e.activation(out=zsil[:, 0:512], in_=u[:, O_ZA:O_ZA + 512], func=AF.Silu), reads=["u"], writes=["zsil"])
        T.op("act", lambda e: e.activation(out=zsil[:, 512:1024], in_=u[:, O_ZB:O_ZB + 512], func=AF.Silu), reads=["u"], writes=["zsil"])
        T.op("dve", lambda e: e.tensor_tensor(out=mixs[:, 0:512], in0=oa_all[:], in1=zsil[:, 0:512], op=ALU.mult), reads=["oa_all", "zsil"], writes=["mixs"])
        for h in range(4):
            T.op("act", lambda e: e.activation(out=junk4[:, 0:128], in_=pOh[:, h, :], func=AF.Square, accum_out=ssh[:, h:h + 1]), reads=["pOh"], writes=["junk4", "ssh"])
        T.op("act", lambda e: e.activation(out=ssh[:, 4:8], in_=ssh[:, 0:4], func=AF.Ln, bias=EPS, scale=1.0 / 128), reads=["ssh"], writes=["ssh"])
        T.op("act", lambda e: e.activation(out=ssh[:, 8:12], in_=ssh[:, 4:8], func=AF.Exp, scale=-0.5), reads=["ssh"], writes=["ssh"])
        T.op("dve", lambda e: e.tensor_tensor(out=ob_s[:], in0=pOh[:], in1=ssh[:, 8:12].unsqueeze(2).to_broadcast([NS, 4, 128]), op=ALU.mult), reads=["pOh", "ssh"], writes=["ob_s"])
        T.op("dve", lambda e: e.tensor_tensor(out=ob_s[:], in0=ob_s[:], in1=gon_s[:].unsqueeze(1).to_broadcast([NS, 4, 128]), op=ALU.mult), reads=["ob_s", "gon_s"], writes=["ob_s"])
        T.op("dve", lambda e: e.tensor_tensor(out=mixs[:, 512:1024], in0=ob_s[:].rearrange("p h v -> p (h v)"), in1=zsil[:, 512:1024], op=ALU.mult),
             reads=["ob_s", "zsil"], writes=["mixs"])
        for kc in range(8):
            T.op("pe", lambda e: e.transpose(pTr[0][:, kc, 0:NS], mixs[:, kc * 128:(kc + 1) * 128], identb[0:NS, 0:NS]), reads=["mixs", "identb"], writes=["pTr0"])
        T.op("act", lambda e: e.activation(out=mixTs[:], in_=pTr[0][:, :, 0:NS], func=AF.Copy), reads=["pTr0"], writes=["mixTs"])
        pY2 = [pSU[:].rearrange("p h v -> p (h v)"), pSc_full[:].rearrange("p s c -> p (s c)")]
        pY2n = ["pSU", "pSc"]
        for hh in range(2):
            for kc in range(8):
                T.op("pe", lambda e: e.matmul(pY2[hh][0:NS, :], lhsT=mixTs[:, kc, :], rhs=wob_s[:, kc, hh * 512:(hh + 1) * 512], start=(kc == 0), stop=(kc == 7)),
                     reads=["mixTs", "wob_s"], writes=[pY2n[hh]])
        for hh in range(2):
            T.op("act", lambda e: e.activation(out=junk4[:], in_=pY2[hh][0:NS, :], func=AF.Square, accum_out=ssh[:, hh:hh + 1]), reads=[pY2n[hh]], writes=["junk4", "ssh"])
        T.op("dve", lambda e: e.tensor_tensor(out=ssh[:, 2:3], in0=ssh[:, 0:1], in1=ssh[:, 1:2], op=ALU.add), reads=["ssh"], writes=["ssh"])
        T.op("act", lambda e: e.activation(out=ssh[:, 3:4], in_=ssh[:, 2:3], func=AF.Ln, bias=EPS, scale=1.0 / D), reads=["ssh"], writes=["ssh"])
        T.op("act", lambda e: e.activation(out=ssh[:, 4:5], in_=ssh[:, 3:4], func=AF.Exp, scale=-0.5), reads=["ssh"], writes=["ssh"])
        for hh in range(2):
            T.op("dve", lambda e: e.scalar_tensor_tensor(out=ysb[:, hh * 512:(hh + 1) * 512], in0=pY2[hh][0:NS, :], scalar=ssh[:, 4:5],
                                                         in1=gpost_s[:, hh * 512:(hh + 1) * 512], op0=ALU.mult, op1=ALU.mult),
                 reads=[pY2n[hh], "ssh", "gpost_s"], writes=["ysb"])
        T.op("dve", lambda e: e.tensor_tensor(out=ysb[:], in0=ysb[:], in1=xs_sb[:], op=ALU.add), reads=["ysb", "xs_sb"], writes=["ysb"])
        T.dma("sp", lambda e: e.dma_start(out=ys_o[:, :], in_=ysb[:]), reads=["ysb"])
        so.close()
        sl.close()
        sg.close()
        sp_.close()
        T.barrier()

    if do_sample:
        for g in range(NGRP):
            sample_group(g)

    T.finish("sp")
    T.close()
    es.close()
    return nc


W_OFF = {"qa": 0, "ka": 512, "va": 1024, "fa": 1536, "za": 1544, "qb": 2056, "fb": 2568, "vb": 3080, "zb": 3592}


def core_inputs(cfg, c, x_prompt, meta_tokens, w_in, b_forget, hgrn_lower_bound, hgrn_out_norm, pre_norm, post_norm, w_out, consts):
    b, j = c // 4, c % 4
    w = w_in[0]
    sl = lambda name: w[:, W_OFF[name] + 128 * j: W_OFF[name] + 128 * j + 128]
    wcore = np.concatenate([sl("qa"), sl("ka"), sl("za"), sl("qb"), sl("zb"), sl("va"),
                            w[:, W_OFF["fa"] + 2 * j: W_OFF["fa"] + 2 * j + 2], sl("fb"), sl("vb")], axis=1)
    CH = min(2048, cfg.SEQ)
    ci, qin, nq = (cfg.TQ * j) // CH, ((cfg.TQ * j) % CH) // cfg.TQ, CH // cfg.TQ
    qidx = ((ci * 1024 + np.arange(8)[None, :] * 128 + np.arange(128)[:, None]) * nq + qin).astype(np.int32)
    return {
        "xp": np.ascontiguousarray(x_prompt[b]),
        "xq": np.ascontiguousarray(x_prompt[b, cfg.TQ * j: cfg.TQ * (j + 1)]),
        "meta": np.ascontiguousarray(meta_tokens),
        "wcore": np.ascontiguousarray(wcore),
        "wout": np.ascontiguousarray(w_out[0]),
        "bf2": np.ascontiguousarray(b_forget[0, 2 * j: 2 * j + 2].reshape(1, 2)),
        "hl_row": np.ascontiguousarray(hgrn_lower_bound[:, 128 * j: 128 * j + 128].reshape(1, 256)),
        "gon": np.ascontiguousarray(hgrn_out_norm[0].reshape(128, 1)),
        "gpre": np.ascontiguousarray(pre_norm[0].reshape(1, D)),
        "gpost": np.ascontiguousarray(post_norm[0].reshape(1, D)),
        "cst": consts,
        "qidx": qidx,
    }


def assemble_prompt(cfg, res):
    B = 2
    L, SEQ = cfg.L, cfg.SEQ
    y = np.zeros((B, SEQ, D), np.float32)
    pk = np.zeros((1, B, L, 8, 64), np.float32)
    pv = np.zeros((1, B, L, 8, 64), np.float32)
    plf = np.zeros((1, B, L, 8), np.float32)
    pS = np.zeros((1, B, 4, 128, 128), np.float32)
    for c in range(8):
        b, j = c // 4, c % 4
        r = res[c]
        y[b, cfg.TQ * j: cfg.TQ * (j + 1)] = r["y_q"]
        pk[0, b, :, 2 * j: 2 * j + 2, :] = r["pkT"].T.reshape(L, 2, 64)
        pv[0, b, :, 2 * j: 2 * j + 2, :] = r["pv"].reshape(L, 2, 64)
        plf[0, b, :, 2 * j: 2 * j + 2] = r["plf"]
        pS[0, b, j] = r["pS"]
    return y, pk, pv, plf, pS


def core_inputs_sample(cfg, c, x_sample, cache_k, cache_v, cache_logf, state_hgrn, page_table, w_in, b_forget,
                       hgrn_lower_bound, hgrn_out_norm, consts2):
    SB = cfg.SB
    lo, hi = SB * c, SB * (c + 1)
    return {
        "xs": np.ascontiguousarray(x_sample[lo:hi].reshape(cfg.NS, D)),
        "state": np.ascontiguousarray(state_hgrn[0, lo:hi]),
        "ptab": np.ascontiguousarray(page_table[lo:hi].reshape(1, SB * cfg.NPG).astype(np.int32)),
        "ck": cache_k[0].reshape(cfg.NPOOL * 32, 2048),
        "cv": cache_v[0].reshape(cfg.NPOOL * 32, 2048),
        "clf": cache_logf[0].reshape(cfg.NPOOL * 32, 32),
        "win": np.ascontiguousarray(w_in[0]),
        "bf8": np.ascontiguousarray(b_forget[0].reshape(1, 8)),
        "hlrows": np.ascontiguousarray(hgrn_lower_bound.reshape(1, 1024)),
        "hlT": np.ascontiguousarray(hgrn_lower_bound.reshape(2, 4, 128).transpose(2, 1, 0).reshape(128, 8)),
        "gonrow": np.ascontiguousarray(hgrn_out_norm[0].reshape(1, 128)),
        "cst2": consts2,
    }


def assemble_sample(cfg, res):
    SB = cfg.SB
    ys = np.concatenate([res[c]["ys"].reshape(SB, 4, D) for c in range(8)], axis=0)
    sk = np.concatenate([res[c]["sk"].reshape(SB, 4, 8, 64) for c in range(8)], axis=0)[None]
    sv = np.concatenate([res[c]["sv"].reshape(SB, 4, 8, 64) for c in range(8)], axis=0)[None]
    slf = np.concatenate([res[c]["slf"].reshape(SB, 4, 8) for c in range(8)], axis=0)[None]
    sS = np.concatenate([res[c]["sS"] for c in range(8)], axis=0)[None]
    return ys, sk, sv, slf, sS


_PROG = {}


def run_all(cfg, inp):
    key = (cfg.NB, cfg.SB, cfg.NPG, cfg.NPOOL)
    if key not in _PROG:
        _PROG[key] = build_program(cfg, do_sample=True)
    nc = _PROG[key]
    consts, consts2 = make_consts(), make_consts2(Cfg(nb=cfg.NB, sb=min(4, cfg.SB), npg=cfg.NPG, npool=cfg.NPOOL))
    ins = []
    for c in range(8):
        d = core_inputs(cfg, c, inp["x_prompt"], inp["meta_tokens"], inp["w_in"], inp["b_forget"], inp["hgrn_lower_bound"],
                        inp["hgrn_out_norm"], inp["pre_norm"], inp["post_norm"], inp["w_out"], consts)
        d.update(core_inputs_sample(cfg, c, inp["x_sample"], inp["cache_k"], inp["cache_v"], inp["cache_logf"], inp["state_hgrn"],
                                    inp["page_table"], inp["w_in"], inp["b_forget"], inp["hgrn_lower_bound"], inp["hgrn_out_norm"], consts2))
        ins.append(d)
    res = run_bass_kernel_spmd(nc, ins, core_ids=list(range(8))).results
    y, pk, pv, plf, pS = assemble_prompt(cfg, res)
    ys, sk, sv, slf, sS = assemble_sample(cfg, res)
    return (y, ys, pk, pv, plf, pS, sk, sv, slf, sS)


def kernel(x_prompt, x_sample, cache_k, cache_v, cache_logf, state_hgrn, page_table, meta_tokens, w_in, b_forget,
           hgrn_lower_bound, hgrn_out_norm, pre_norm, post_norm, w_out):
    inp = dict(x_prompt=x_prompt, x_sample=x_sample, cache_k=cache_k, cache_v=cache_v, cache_logf=cache_logf,
               state_hgrn=state_hgrn, page_table=page_table, meta_tokens=meta_tokens, w_in=w_in, b_forget=b_forget,
               hgrn_lower_bound=hgrn_lower_bound, hgrn_out_norm=hgrn_out_norm, pre_norm=pre_norm, post_norm=post_norm, w_out=w_out)
    inp = {k: np.asarray(v) for k, v in inp.items()}
    cfg = Cfg(nb=x_prompt.shape[1] // 128, sb=x_sample.shape[0] // 8, npg=page_table.shape[1], npool=cache_k.shape[1])
    return run_all(cfg, inp)
```

```python
import contextlib
import os
import numpy as np
import ml_dtypes
import concourse.bass as bass
import concourse.mybir as mybir
from concourse.bass_utils import run_bass_kernel_spmd

F32 = mybir.dt.float32
BF16 = mybir.dt.bfloat16
I32 = mybir.dt.int32
AF = mybir.ActivationFunctionType
ALU = mybir.AluOpType
AX = mybir.AxisListType

D = 1024
NMETA = 16
EPS = 1e-6
SCALE = 64 ** -0.5


class Cfg:
    def __init__(self, nb=64, sb=16, npg=16, npool=2560):
        self.NB = nb
        self.SEQ = 128 * nb
        self.L = NMETA + self.SEQ
        self.NT = nb // 4
        self.TQ = self.SEQ // 4
        self.SB = sb
        self.NS = 4 * sb
        self.NPG = npg
        self.PAST = 128 * npg
        self.NPOOL = npool


class Trk:
    def __init__(self, nc, n_dma_sems=40):
        self.nc = nc
        self.engs = {"pe": nc.tensor, "act": nc.scalar, "dve": nc.vector, "pool": nc.gpsimd, "sp": nc.sync}
        self.sem, self.cnt, self.seen, self.last_w, self.readers = {}, {}, {}, {}, {}
        self._ctx = []
        for k in self.engs:
            cm = nc.semaphore("s_" + k)
            self.sem[k] = cm.__enter__()
            self._ctx.append(cm)
            self.cnt[k] = 0
        self.dma_sems = []
        for i in range(n_dma_sems):
            cm = nc.semaphore("s_dma%d" % i)
            s = cm.__enter__()
            self._ctx.append(cm)
            self.dma_sems.append([s, 0])
        self.dma_rr = 0
        cm = nc.semaphore("s_cc")
        self.cc_sem = cm.__enter__()
        self._ctx.append(cm)
        self.cc_cnt = 0
        self.n_wait = 0

    def close(self):
        for cm in reversed(self._ctx):
            cm.__exit__(None, None, None)

    def _wait(self, e, tok):
        if tok is None:
            return
        key, sem, c = tok
        if key == "pe" and e == "pe":
            return
        if self.seen.get((e, key), 0) >= c:
            return
        self.engs[e].wait_ge(sem, c)
        if os.environ.get("K_TRACE"):
            print("   WAIT", e, "on", key, c)
        self.n_wait += 1
        self.seen[(e, key)] = c

    def _deps(self, e, reads, writes):
        for b in reads:
            for t in self.last_w.get(b, ()):
                self._wait(e, t)
        for b in writes:
            for t in self.last_w.get(b, ()):
                self._wait(e, t)
            for t in self.readers.get(b, ()):
                self._wait(e, t)

    def _commit(self, tok, reads, writes):
        for b in reads:
            lst = self.readers.setdefault(b, [])
            lst[:] = [t for t in lst if t[0] != tok[0]] + [tok]
        for b in writes:
            lst = self.last_w.setdefault(b, [])
            lst[:] = [t for t in lst if t[0] != tok[0]] + [tok]
            self.readers[b] = []

    def _skip(self):
        self.n_emit = getattr(self, "n_emit", 0) + 1
        lim = int(os.environ.get("K_LIMIT", "0"))
        if os.environ.get("K_TRACE"):
            import traceback
            fr = traceback.extract_stack()[-3]
            print("OP", self.n_emit, fr.lineno, fr.line[:110])
        if str(self.n_emit) in os.environ.get("K_SKIP", "").split(","):
            return True
        return lim > 0 and self.n_emit > lim

    def op(self, e, fn, reads=(), writes=()):
        if self._skip():
            return None
        if e != "pe":
            writes = list(writes) + [b for b in reads if b.startswith("p") and b[1:2].isupper()]
        self._deps(e, reads, writes)
        inst = fn(self.engs[e])
        self.cnt[e] += 1
        inst.then_inc(self.sem[e], 1)
        tok = (e, self.sem[e], self.cnt[e])
        self._commit(tok, reads, writes)
        return tok

    def dma(self, e, fn, reads=(), writes=()):
        if self._skip():
            return None
        self._deps(e, reads, writes)
        i = self.dma_rr
        self.dma_rr = (self.dma_rr + 1) % len(self.dma_sems)
        ent = self.dma_sems[i]
        key = "dma%d" % i
        if ent[1] > 0:
            self._wait(e, (key, ent[0], ent[1]))
        inst = fn(self.engs[e])
        ent[1] += 16
        inst.then_inc(ent[0], 16)
        tok = (key, ent[0], ent[1])
        self._commit(tok, reads, writes)
        return tok

    def cc(self, fn, reads=(), writes=()):
        e = "pool"
        self._deps(e, reads, writes)
        if self.cc_cnt > 0:
            self._wait(e, ("cc", self.cc_sem, self.cc_cnt))
        inst = fn(self.engs[e])
        self.cc_cnt += 1
        inst.then_inc(self.cc_sem)
        tok = ("cc", self.cc_sem, self.cc_cnt)
        self._commit(tok, reads, writes)
        return tok

    def barrier(self):
        toks = [(k, self.sem[k], self.cnt[k]) for k in self.engs if self.cnt[k] > 0]
        toks += [("dma%d" % i, ent[0], ent[1]) for i, ent in enumerate(self.dma_sems) if ent[1] > 0]
        if self.cc_cnt:
            toks.append(("cc", self.cc_sem, self.cc_cnt))
        for e in self.engs:
            for t in toks:
                if t[0] != e:
                    self._wait(e, t)

    def finish(self, e="sp"):
        for i, ent in enumerate(self.dma_sems):
            if ent[1] > 0:
                self._wait(e, ("dma%d" % i, ent[0], ent[1]))
        if self.cc_cnt:
            self._wait(e, ("cc", self.cc_sem, self.cc_cnt))
        for k in self.engs:
            if k != e and self.cnt[k] > 0:
                self._wait(e, (k, self.sem[k], self.cnt[k]))


C_ID, C_TRI, C_TQ, C_TAUX, C_TM = 0, 128, 256, 384, 387
C_TQ16, C_TAUX16, C_TM16, C_SEL, C_ONES, C_END = 515, 531, 534, 550, 678, 806


def make_consts():
    c = np.zeros((128, C_END), np.float32)
    s = np.arange(128)[:, None]
    t = np.arange(128)[None, :]
    c[:, C_ID:C_ID + 128] = (s == t)
    c[:, C_TRI:C_TRI + 128] = (s <= t)
    c[:, C_TQ:C_TQ + 128] = (s <= t).astype(np.float32) - (s <= 63)
    c[:, C_TAUX + 0] = (s[:, 0] <= 63)
    c[:, C_TAUX + 1] = 1.0
    c[:, C_TAUX + 2] = (s[:, 0] > 63)
    c[:, C_TM:C_TM + 128] = (s <= 63).astype(np.float32) - (s <= t)
    s16 = np.arange(16)[:, None]
    t16 = np.arange(16)[None, :]
    c[:16, C_TQ16:C_TQ16 + 16] = (s16 <= t16).astype(np.float32) - (s16 <= 7)
    c[:16, C_TAUX16 + 0] = (s16[:, 0] <= 7)
    c[:16, C_TAUX16 + 1] = 1.0
    c[:16, C_TAUX16 + 2] = (s16[:, 0] > 7)
    c[:16, C_TM16:C_TM16 + 16] = (s16 <= 7).astype(np.float32) - (s16 <= t16)
    c[64, C_SEL:C_SEL + 64] = 1.0
    c[0, C_SEL + 64:C_SEL + 128] = 1.0
    c[:, C_ONES:C_ONES + 128] = 1.0
    return c


def consts2_layout(cfg):
    SB, NS = cfg.SB, cfg.NS
    lay, o = {}, 0
    for name, w in [("BT", 128), ("BU", 128), ("MNEW", SB * 32), ("BMT", SB * NS), ("BM", SB), ("SUP", 128), ("SELP", 4), ("GCOL", 1)]:
        lay[name] = o
        o += w
    lay["END"] = o
    return lay


def make_consts2(cfg):
    SB, NS = cfg.SB, cfg.NS
    lay = consts2_layout(cfg)
    c = np.zeros((128, lay["END"]), np.float32)
    t = np.arange(NS)
    same = (t[:, None] // 4) == (t[None, :] // 4)
    c[:NS, lay["BT"]:lay["BT"] + NS] = same & (t[:, None] <= t[None, :])
    c[:NS, lay["BU"]:lay["BU"] + NS] = same & (t[:, None] > t[None, :])
    m = np.zeros((NS, SB, 8, 4), np.float32)
    for tok in range(NS):
        b, jq = tok // 4, tok % 4
        m[tok, b, :, jq:] = 1.0
    c[:NS, lay["MNEW"]:lay["MNEW"] + SB * 32] = m.reshape(NS, SB * 32)
    bmt = np.zeros((SB, NS), np.float32)
    for b in range(SB):
        bmt[b, 4 * b:4 * b + 4] = 1.0
    c[:, lay["BMT"]:lay["BMT"] + SB * NS] = bmt.reshape(1, SB * NS)
    c[:NS, lay["BM"]:lay["BM"] + SB] = bmt.T
    p = np.arange(128)
    c[:, lay["SUP"]:lay["SUP"] + 128] = (p[:, None] > p[None, :])
    c[:, lay["SELP"]:lay["SELP"] + 4] = (p[:, None] // 32) == np.arange(4)[None, :]
    c[:, lay["GCOL"]] = p % 32
    return c


W_QA, W_KA, W_ZA, W_QB, W_ZB, W_T = 0, 128, 256, 384, 512, 640
WT_VA, WT_FA, WT_FB, WT_VB, WT_N = 0, 128, 130, 258, 386
WCORE_N = W_T + WT_N


def build_program(cfg, do_sample=True):
    nc = bass.Bass("TRN2", target_bir_lowering=False)
    L, SEQ, NB, NT, TQ = cfg.L, cfg.SEQ, cfg.NB, cfg.NT, cfg.TQ

    def din(name, shape, dt=F32):
        return nc.dram_tensor(name, list(shape), dt, kind="ExternalInput").ap()

    def dout(name, shape, dt=F32):
        return nc.dram_tensor(name, list(shape), dt, kind="ExternalOutput").ap()

    xp = din("xp", [SEQ, D])
    xq = din("xq", [TQ, D])
    meta = din("meta", [NMETA, D])
    wcore = din("wcore", [D, WCORE_N])
    wout = din("wout", [D, D])
    bf2 = din("bf2", [1, 2])
    hl_row = din("hl_row", [1, 256])
    gon = din("gon", [128, 1])
    gpre = din("gpre", [1, D])
    gpost = din("gpost", [1, D])
    cst_d = din("cst", [128, C_END])
    qidx = din("qidx", [128, 8], I32)

    y_q = dout("y_q", [TQ, D])
    pkT = dout("pkT", [128, L])
    pv = dout("pv", [L, 128])
    plf = dout("plf", [L, 2])
    pS = dout("pS", [128, 128])

    CH = min(2048, SEQ)
    NCH = SEQ // CH
    mix_src = nc.dram_tensor("mix_src", [NCH * 256, CH], BF16)
    mix_dst = nc.dram_tensor("mix_dst", [NCH * 1024, CH], BF16)

    def mix_src_ap(half, c0):
        ci, off = c0 // CH, c0 % CH
        return mix_src[ci * 256 + 128 * half:ci * 256 + 128 * half + 128, off:off + 512]

    c2 = consts2_layout(Cfg(nb=cfg.NB, sb=min(4, cfg.SB), npg=cfg.NPG, npool=cfg.NPOOL))
    es = contextlib.ExitStack()
    es.enter_context(nc.allow_non_contiguous_dma(reason="small strided parameter loads / column stores"))
    T = Trk(nc)

    sfx = [""]

    def sb(stack, name, shape, dt=F32):
        return stack.enter_context(nc.sbuf_tensor("sb_" + name + sfx[0], list(shape), dt))

    def ps(stack, name, shape, dt=F32):
        return stack.enter_context(nc.psum_tensor("ps_" + name + sfx[0], list(shape), dt))

    cst = sb(es, "cst", [128, C_END])
    identb = sb(es, "identb", [128, 128], BF16)
    onesb = sb(es, "onesb", [128, 128], BF16)
    pa = contextlib.ExitStack()
    KT = sb(pa, "KT", [128, L], BF16)
    QT = sb(pa, "QT", [128, L], BF16)
    zaT = sb(pa, "zaT", [128, SEQ], BF16)
    Vaug = sb(pa, "Vaug", [128, NB + 1, 2, 128], BF16)
    negc = sb(pa, "negc", [128, NB + 1, 2])
    rq = sb(pa, "rq", [128, NT + 1, 2])

    T.dma("sp", lambda e: e.dma_start(out=cst[:], in_=cst_d[:, :]), writes=["cst"])
    T.dma("pool", lambda e: e.dma_start(out=identb[:], in_=cst_d[:, C_ID:C_ID + 128]), writes=["identb"])
    T.op("dve", lambda e: e.memset(onesb[:], 1.0), writes=["onesb"])
    T.op("pool", lambda e: e.memset(Vaug[:], 0.0), writes=["Vaug"])
    T.op("pool", lambda e: e.memset(Vaug[:, :, 0, 64:65], 1.0), writes=["Vaug"])
    T.op("pool", lambda e: e.memset(Vaug[:, :, 1, 0:1], 1.0), writes=["Vaug"])
    tri = cst[:, C_TRI:C_TRI + 128]

    p1 = contextlib.ExitStack()
    wcb = sb(p1, "wcb", [128, 8, WCORE_N], BF16)
    gpre_bc = sb(p1, "gpre_bc", [128, D])
    bf_bc = sb(p1, "bf_bc", [128, 2])
    lb_bc = sb(p1, "lb_bc", [128, 128])
    oml_bc = sb(p1, "oml_bc", [128, 128])
    gon_sb = sb(p1, "gon_sb", [128, 1])
    hl_bc = sb(p1, "hl_bc", [128, 2, 128])
    S0 = sb(p1, "S0", [128, 128])
    S0t = sb(p1, "S0t", [128, 128])
    S0b = sb(p1, "S0b", [128, 128], BF16)
    ctot = sb(p1, "ctot", [128, 2])
    xblk = [sb(p1, "xblk%d" % i, [128, D]) for i in range(2)]
    junk = sb(p1, "junk", [128, D], BF16)
    ss = sb(p1, "ss", [128, 4])
    hn = [sb(p1, "hn%d" % i, [128, D], BF16) for i in range(2)]
    hnT = sb(p1, "hnT", [128, 8, 512], BF16)
    KTf = sb(p1, "KTf", [128, 512])
    vaf = sb(p1, "vaf", [128, 4, 128])
    qbT = sb(p1, "qbT", [128, 512])
    zbT = sb(p1, "zbT", [128, 512], BF16)
    vbb = sb(p1, "vbb", [128, 4, 128], BF16)
    fa_sb = sb(p1, "fa_sb", [128, 4, 2])
    lf_sb = sb(p1, "lf_sb", [128, 4, 2])
    sig = sb(p1, "sig", [128, 4, 128])
    gg = sb(p1, "gg", [128, 4, 128])
    logg = sb(p1, "logg", [128, 4, 128])
    kb = sb(p1, "kb", [128, 4, 128])
    EK = sb(p1, "EK", [128, 4, 128])
    EQ = sb(p1, "EQ", [128, 4, 128])
    e3 = sb(p1, "e3", [128, 4, 3])
    K2 = sb(p1, "K2", [128, 4, 128], BF16)
    K2T = sb(p1, "K2T", [128, 4, 128], BF16)
    Q2T = sb(p1, "Q2T", [128, 4, 128], BF16)
    ATm = sb(p1, "ATm", [128, 4, 128], BF16)
    oT = sb(p1, "oT", [128, 512])
    sq = sb(p1, "sq", [128, 512], BF16)
    rstd = sb(p1, "rstd", [128, 512])
    mixB = sb(p1, "mixB", [128, 512], BF16)
    pT = ps(p1, "pT", [128, 8, 128], BF16)
    pF = [ps(p1, "pF%d" % i, [128, 512]) for i in range(2)]
    pTf = [ps(p1, "pTf%d" % i, [128, 512]) for i in range(2)]
    pH1 = ps(p1, "pH1", [128, 4, 128])
    pH2 = ps(p1, "pH2", [128, 4, 128])
    pH3 = ps(p1, "pH3", [128, 256])
    pK2T = pT

    T.dma("pool", lambda e: e.dma_start(out=wcb[:], in_=wcore.rearrange("(c p) n -> p c n", p=128)), writes=["wcb"])
    T.dma("sp", lambda e: e.dma_start(out=gpre_bc[:], in_=gpre.partition_broadcast(128)), writes=["gpre_bc"])
    T.dma("sp", lambda e: e.dma_start(out=bf_bc[:], in_=bf2.partition_broadcast(128)), writes=["bf_bc"])
    T.dma("sp", lambda e: e.dma_start(out=gon_sb[:], in_=gon[:, :]), writes=["gon_sb"])
    T.dma("sp", lambda e: e.dma_start(out=hl_bc[:].rearrange("p a b -> p (a b)"), in_=hl_row.partition_broadcast(128)), writes=["hl_bc"])
    T.op("dve", lambda e: e.tensor_tensor(out=lb_bc[:], in0=hl_bc[:, 0, :], in1=hl_bc[:, 1, :], op=ALU.subtract), reads=["hl_bc"], writes=["lb_bc"])
    T.op("act", lambda e: e.activation(out=lb_bc[:], in_=lb_bc[:], func=AF.Sigmoid), reads=["lb_bc"], writes=["lb_bc"])
    T.op("dve", lambda e: e.tensor_scalar(out=oml_bc[:], in0=lb_bc[:], scalar1=-1.0, scalar2=1.0, op0=ALU.mult, op1=ALU.add), reads=["lb_bc"], writes=["oml_bc"])
    T.op("dve", lambda e: e.memset(S0[:], 0.0), writes=["S0"])
    T.op("dve", lambda e: e.memset(ctot[:], 0.0), writes=["ctot"])

    blk_i = [0]

    def tile_phase1(tt):
        if tt == 0:
            n, nblk, bs, tok0 = NMETA, 1, NMETA, 0
        else:
            n, nblk, bs, tok0 = 512, 4, 128, NMETA + 512 * (tt - 1)
        gb0 = 0 if tt == 0 else 1 + 4 * (tt - 1)
        for bl in range(nblk):
            k = blk_i[0] % 2
            blk_i[0] += 1
            xb, hb = xblk[k], hn[k]
            xbn, hbn = "xblk%d" % k, "hn%d" % k
            if tt == 0:
                src = meta[:, :]
            else:
                r0 = 512 * (tt - 1) + 128 * bl
                src = xp[r0:r0 + 128, :]
            T.dma("sp", lambda e: e.dma_start(out=xb[0:bs, :], in_=src), writes=[xbn])
            T.op("act", lambda e: e.activation(out=junk[0:bs, :], in_=xb[0:bs, :], func=AF.Square, accum_out=ss[0:bs, 0:1]),
                 reads=[xbn], writes=["junk", "ss"])
            T.op("act", lambda e: e.activation(out=ss[0:bs, 1:2], in_=ss[0:bs, 0:1], func=AF.Ln, bias=EPS, scale=1.0 / D), reads=["ss"], writes=["ss"])
            T.op("act", lambda e: e.activation(out=ss[0:bs, 2:3], in_=ss[0:bs, 1:2], func=AF.Exp, scale=-0.5), reads=["ss"], writes=["ss"])
            T.op("dve", lambda e: e.scalar_tensor_tensor(out=hb[0:bs, :], in0=xb[0:bs, :], scalar=ss[0:bs, 2:3], in1=gpre_bc[0:bs, :],
                                                         op0=ALU.mult, op1=ALU.mult), reads=[xbn, "ss", "gpre_bc"], writes=[hbn])
            for kc in range(8):
                T.op("pe", lambda e: e.transpose(pT[:, kc, 0:bs], hb[0:bs, kc * 128:(kc + 1) * 128], identb[0:bs, 0:bs]),
                     reads=[hbn, "identb"], writes=["pT"])
            T.op("act", lambda e: e.activation(out=hnT[:, :, bl * 128:bl * 128 + bs], in_=pT[:, :, 0:bs], func=AF.Copy), reads=["pT"], writes=["hnT"])
        fi = [0]

        def fform(c0):
            pf = pF[fi[0] % 2]
            nm = "pF%d" % (fi[0] % 2)
            fi[0] += 1
            for kc in range(8):
                T.op("pe", lambda e: e.matmul(pf[:, 0:n], lhsT=wcb[:, kc, c0:c0 + 128], rhs=hnT[:, kc, 0:n], start=(kc == 0), stop=(kc == 7)),
                     reads=["wcb", "hnT"], writes=[nm])
            return pf, nm

        pf, nm = fform(W_QA)
        T.op("dve", lambda e: e.tensor_copy(out=QT[:, tok0:tok0 + n], in_=pf[:, 0:n]), reads=[nm], writes=["QT"])
        pf, nm = fform(W_KA)
        T.op("act", lambda e: e.activation(out=KTf[:, 0:n], in_=pf[:, 0:n], func=AF.Copy), reads=[nm], writes=["KTf"])
        T.op("dve", lambda e: e.tensor_copy(out=KT[:, tok0:tok0 + n], in_=pf[:, 0:n]), reads=[nm], writes=["KT"])
        T.dma("sp", lambda e: e.dma_start(out=pkT[:, tok0:tok0 + n], in_=KTf[:, 0:n]), reads=["KTf"])
        if tt > 0:
            pf, nm = fform(W_ZA)
            T.op("act", lambda e: e.activation(out=zaT[:, tok0 - NMETA:tok0 - NMETA + n], in_=pf[:, 0:n], func=AF.Silu), reads=[nm], writes=["zaT"])
        pf, nm = fform(W_QB)
        T.op("act", lambda e: e.activation(out=qbT[:, 0:n], in_=pf[:, 0:n], func=AF.Silu), reads=[nm], writes=["qbT"])
        if tt > 0:
            pf, nm = fform(W_ZB)
            T.op("act", lambda e: e.activation(out=zbT[:, 0:n], in_=pf[:, 0:n], func=AF.Silu), reads=[nm], writes=["zbT"])
        for bl in range(nblk):
            pt_ = pTf[bl % 2]
            nm = "pTf%d" % (bl % 2)
            for kc in range(8):
                T.op("pe", lambda e: e.matmul(pt_[0:bs, 0:WT_N], lhsT=hnT[:, kc, bl * 128:bl * 128 + bs], rhs=wcb[:, kc, W_T:W_T + WT_N],
                                              start=(kc == 0), stop=(kc == 7)), reads=["hnT", "wcb"], writes=[nm])
            gb = gb0 + bl
            T.op("act", lambda e: e.activation(out=vaf[0:bs, bl, :], in_=pt_[0:bs, WT_VA:WT_VA + 128], func=AF.Copy), reads=[nm], writes=["vaf"])
            T.op("dve", lambda e: e.tensor_copy(out=Vaug[0:bs, gb, 0, 0:64], in_=pt_[0:bs, WT_VA:WT_VA + 64]), reads=[nm], writes=["Vaug"])
            T.op("dve", lambda e: e.tensor_copy(out=Vaug[0:bs, gb, 1, 64:128], in_=pt_[0:bs, WT_VA + 64:WT_VA + 128]), reads=[nm], writes=["Vaug"])
            T.op("dve", lambda e: e.tensor_tensor(out=fa_sb[0:bs, bl, :], in0=pt_[0:bs, WT_FA:WT_FA + 2], in1=bf_bc[0:bs, :], op=ALU.add),
                 reads=[nm, "bf_bc"], writes=["fa_sb"])
            T.op("act", lambda e: e.activation(out=sig[0:bs, bl, :], in_=pt_[0:bs, WT_FB:WT_FB + 128], func=AF.Sigmoid), reads=[nm], writes=["sig"])
            T.op("dve", lambda e: e.tensor_copy(out=vbb[0:bs, bl, :], in_=pt_[0:bs, WT_VB:WT_VB + 128]), reads=[nm], writes=["vbb"])
        if tt == 0:
            T.dma("sp", lambda e: e.dma_start(out=pv[0:NMETA, :], in_=vaf[0:NMETA, 0, :]), reads=["vaf"])
        else:
            T.dma("sp", lambda e: e.dma_start(out=pv[tok0:tok0 + n, :].rearrange("(b p) d -> p b d", p=128), in_=vaf[:, :, :]), reads=["vaf"])
        T.op("act", lambda e: e.activation(out=lf_sb[0:bs, 0:nblk, :], in_=fa_sb[0:bs, 0:nblk, :], func=AF.Exp, scale=-1.0), reads=["fa_sb"], writes=["lf_sb"])
        T.op("act", lambda e: e.activation(out=lf_sb[0:bs, 0:nblk, :], in_=lf_sb[0:bs, 0:nblk, :], func=AF.Ln, bias=1.0, scale=1.0), reads=["lf_sb"], writes=["lf_sb"])
        T.op("dve", lambda e: e.tensor_scalar(out=lf_sb[0:bs, 0:nblk, :], in0=lf_sb[0:bs, 0:nblk, :], scalar1=-1.0, scalar2=None, op0=ALU.mult),
             reads=["lf_sb"], writes=["lf_sb"])
        if tt == 0:
            T.dma("sp", lambda e: e.dma_start(out=plf[0:NMETA, :], in_=lf_sb[0:NMETA, 0, :]), reads=["lf_sb"])
        else:
            T.dma("sp", lambda e: e.dma_start(out=plf[tok0:tok0 + n, :].rearrange("(b p) h -> p b h", p=128), in_=lf_sb[:, :, :]), reads=["lf_sb"])
        for bl in range(nblk):
            gb = gb0 + bl
            T.op("pe", lambda e: e.matmul(pH3[0:bs, 16:18], lhsT=cst[0:bs, C_TRI:C_TRI + bs], rhs=lf_sb[0:bs, bl, :], start=True, stop=True),
                 reads=["cst", "lf_sb"], writes=["pH3c"])
            T.op("pe", lambda e: e.matmul(pH3[:, 18:20], lhsT=cst[0:bs, C_ONES:C_ONES + 128], rhs=lf_sb[0:bs, bl, :], start=True, stop=True),
                 reads=["cst", "lf_sb"], writes=["pH3c"])
            if tt > 0 and bl == 2:
                T.op("dve", lambda e: e.tensor_copy(out=rq[:, tt, :], in_=ctot[:]), reads=["ctot"], writes=["rq"])
            T.op("dve", lambda e: e.scalar_tensor_tensor(out=negc[0:bs, gb, :], in0=pH3[0:bs, 16:18], scalar=-1.0, in1=ctot[0:bs, :],
                                                         op0=ALU.mult, op1=ALU.subtract), reads=["pH3c", "ctot"], writes=["negc"])
            T.op("dve", lambda e: e.tensor_tensor(out=ctot[:], in0=ctot[:], in1=pH3[:, 18:20], op=ALU.add), reads=["pH3c", "ctot"], writes=["ctot"])
        T.op("dve", lambda e: e.tensor_tensor(out=gg[0:bs, 0:nblk, :], in0=sig[0:bs, 0:nblk, :], in1=oml_bc[0:bs, :].unsqueeze(1).to_broadcast([bs, nblk, 128]), op=ALU.mult),
             reads=["sig", "oml_bc"], writes=["gg"])
        T.op("dve", lambda e: e.tensor_tensor(out=gg[0:bs, 0:nblk, :], in0=gg[0:bs, 0:nblk, :], in1=lb_bc[0:bs, :].unsqueeze(1).to_broadcast([bs, nblk, 128]), op=ALU.add),
             reads=["gg", "lb_bc"], writes=["gg"])
        T.op("act", lambda e: e.activation(out=logg[0:bs, 0:nblk, :], in_=gg[0:bs, 0:nblk, :], func=AF.Ln), reads=["gg"], writes=["logg"])
        T.op("dve", lambda e: e.tensor_scalar(out=kb[0:bs, 0:nblk, :], in0=gg[0:bs, 0:nblk, :], scalar1=-1.0, scalar2=1.0, op0=ALU.mult, op1=ALU.add),
             reads=["gg"], writes=["kb"])
        if tt == 0:
            ctq, cta, ctm = C_TQ16, C_TAUX16, C_TM16
        else:
            ctq, cta, ctm = C_TQ, C_TAUX, C_TM
        for bl in range(nblk):
            T.op("pe", lambda e: e.matmul(pH1[:, bl, 0:bs], lhsT=logg[0:bs, bl, :], rhs=cst[0:bs, ctq:ctq + bs], start=True, stop=True),
                 reads=["logg", "cst"], writes=["pH1"])
            T.op("pe", lambda e: e.matmul(pH3[:, 3 * bl:3 * bl + 3], lhsT=logg[0:bs, bl, :], rhs=cst[0:bs, cta:cta + 3], start=True, stop=True),
                 reads=["logg", "cst"], writes=["pH3a"])
            T.op("pe", lambda e: e.matmul(pH2[0:bs, bl, :], lhsT=cst[0:bs, ctm:ctm + bs], rhs=logg[0:bs, bl, :], start=True, stop=True),
                 reads=["logg", "cst"], writes=["pH2"])
        T.op("act", lambda e: e.activation(out=EQ[:, 0:nblk, 0:bs], in_=pH1[:, 0:nblk, 0:bs], func=AF.Exp), reads=["pH1"], writes=["EQ"])
        T.op("act", lambda e: e.activation(out=e3[:, 0:nblk, :], in_=pH3[:, 0:3 * nblk].rearrange("p (b c) -> p b c", c=3), func=AF.Exp), reads=["pH3a"], writes=["e3"])
        T.op("act", lambda e: e.activation(out=EK[0:bs, 0:nblk, :], in_=pH2[0:bs, 0:nblk, :], func=AF.Exp), reads=["pH2"], writes=["EK"])
        T.op("dve", lambda e: e.tensor_tensor(out=Q2T[:, 0:nblk, 0:bs], in0=qbT[:, 0:n].rearrange("p (b t) -> p b t", t=bs), in1=EQ[:, 0:nblk, 0:bs], op=ALU.mult),
             reads=["qbT", "EQ"], writes=["Q2T"])
        T.op("dve", lambda e: e.tensor_tensor(out=K2[0:bs, 0:nblk, :], in0=kb[0:bs, 0:nblk, :], in1=EK[0:bs, 0:nblk, :], op=ALU.mult),
             reads=["kb", "EK"], writes=["K2"])
        for bl in range(nblk):
            T.op("pe", lambda e: e.transpose(pK2T[:, bl, 0:bs], K2[0:bs, bl, :], identb[0:bs, 0:bs]), reads=["K2", "identb"], writes=["pT"])
        T.op("act", lambda e: e.activation(out=K2T[:, 0:nblk, 0:bs], in_=pK2T[:, 0:nblk, 0:bs], func=AF.Copy), reads=["pT"], writes=["K2T"])
        for bl in range(nblk):
            T.op("pe", lambda e: e.matmul(pH2[0:bs, bl, 0:bs], lhsT=K2T[:, bl, 0:bs], rhs=Q2T[:, bl, 0:bs], start=True, stop=True),
                 reads=["K2T", "Q2T"], writes=["pH2"])
        T.op("dve", lambda e: e.tensor_tensor(out=ATm[0:bs, 0:nblk, 0:bs], in0=pH2[0:bs, 0:nblk, 0:bs],
                                              in1=cst[0:bs, C_TRI:C_TRI + bs].unsqueeze(1).to_broadcast([bs, nblk, bs]), op=ALU.mult),
             reads=["pH2", "cst"], writes=["ATm"])
        for bl in range(nblk):
            T.op("dve", lambda e: e.tensor_scalar(out=S0b[:], in0=S0[:], scalar1=e3[:, bl, 0:1], scalar2=None, op0=ALU.mult), reads=["S0", "e3"], writes=["S0b"])
            T.op("pe", lambda e: e.matmul(pH1[:, bl, 0:bs], lhsT=vbb[0:bs, bl, :], rhs=ATm[0:bs, bl, 0:bs], start=True, stop=False),
                 reads=["vbb", "ATm"], writes=["pH1"])
            T.op("pe", lambda e: e.matmul(pH1[:, bl, 0:bs], lhsT=S0b[:], rhs=Q2T[:, bl, 0:bs], start=False, stop=True),
                 reads=["S0b", "Q2T"], writes=["pH1"])
            T.op("pe", lambda e: e.matmul(pH3[:, 128:256], lhsT=K2[0:bs, bl, :], rhs=vbb[0:bs, bl, :], start=True, stop=True),
                 reads=["K2", "vbb"], writes=["pH3s"])
            T.op("dve", lambda e: e.tensor_scalar(out=S0t[:], in0=S0[:], scalar1=e3[:, bl, 1:2], scalar2=None, op0=ALU.mult), reads=["S0", "e3"], writes=["S0t"])
            T.op("dve", lambda e: e.scalar_tensor_tensor(out=S0[:], in0=pH3[:, 128:256], scalar=e3[:, bl, 2:3], in1=S0t[:], op0=ALU.mult, op1=ALU.add),
                 reads=["pH3s", "e3", "S0t"], writes=["S0"])
        if tt > 0:
            pH1f = pH1[:].rearrange("p b t -> p (b t)")
            T.op("act", lambda e: e.activation(out=sq[:], in_=pH1f, func=AF.Square), reads=["pH1"], writes=["sq"])
            T.op("dve", lambda e: e.tensor_copy(out=oT[:], in_=pH1f), reads=["pH1"], writes=["oT"])
            pSS = pH2[:].rearrange("p b t -> p (b t)")
            T.op("pe", lambda e: e.matmul(pSS, lhsT=onesb[:], rhs=sq[:], start=True, stop=True), reads=["onesb", "sq"], writes=["pH2"])
            T.op("act", lambda e: e.activation(out=rstd[:], in_=pSS, func=AF.Ln, bias=EPS, scale=1.0 / 128), reads=["pH2"], writes=["rstd"])
            T.op("act", lambda e: e.activation(out=rstd[:], in_=rstd[:], func=AF.Exp, scale=-0.5), reads=["rstd"], writes=["rstd"])
            T.op("dve", lambda e: e.tensor_tensor(out=oT[:], in0=oT[:], in1=rstd[:], op=ALU.mult), reads=["oT", "rstd"], writes=["oT"])
            T.op("dve", lambda e: e.scalar_tensor_tensor(out=mixB[:], in0=oT[:], scalar=gon_sb[:, 0:1], in1=zbT[:], op0=ALU.mult, op1=ALU.mult),
                 reads=["oT", "gon_sb", "zbT"], writes=["mixB"])
            c0 = tok0 - NMETA
            T.dma("sp", lambda e: e.dma_start(out=mix_src_ap(1, c0), in_=mixB[:]), reads=["mixB"], writes=["mix_src"])

    def early_exit(*stacks):
        for st in stacks:
            st.close()
        pa.close()
        T.finish("sp")
        T.close()
        es.close()
        return nc

    for tt in range(NT + 1):
        tile_phase1(tt)
        if os.environ.get("K_STOP") == "t%d" % tt:
            return early_exit(p1)
    T.dma("sp", lambda e: e.dma_start(out=pS[:, :], in_=S0[:]), reads=["S0"])
    p1.close()
    if os.environ.get("K_STOP") == "p1":
        return early_exit()
    T.barrier()

    p2 = contextlib.ExitStack()
    NPB = 4
    Pb = [[sb(p2, "P%d_%d" % (h, i), [128, 512], BF16) for i in range(NPB)] for h in range(2)]
    bias = sb(p2, "bias", [128, 2, NB + 1])
    Osb = sb(p2, "Osb", [128, 512])
    rsum = sb(p2, "rsum", [128, 512])
    mixA = sb(p2, "mixA", [128, 512], BF16)
    pS_ = [[ps(p2, "pS%d_%d" % (h, i), [128, 512]) for i in range(2)] for h in range(2)]
    pO = [ps(p2, "pO%d" % h, [128, 512]) for h in range(2)]
    pRB = ps(p2, "pRB", [128, 512])
    cnt = [0]
    T.op("dve", lambda e: e.memset(rsum[:], 0.0), writes=["rsum"])
    for Q in range(1, NT + 1):
        q0 = NMETA + 512 * (Q - 1)
        nfull = 1 + 4 * (Q - 1)
        nblocks = nfull + 4
        for h in range(2):
            T.op("dve", lambda e: e.tensor_scalar(out=bias[:, h, 0:nblocks], in0=negc[:, 0:nblocks, h], scalar1=rq[:, Q, h:h + 1], scalar2=None, op0=ALU.add),
                 reads=["negc", "rq"], writes=["bias"])

        def blk_geom(j):
            if j == 0:
                return 0, NMETA, 0
            k0 = NMETA + 128 * (j - 1)
            jl = j - nfull
            return k0, 128, (128 * jl if jl > 0 else 0)

        def emit_qk(j):
            k0, bs, qc = blk_geom(j)
            i = cnt[0] + j
            for h in range(2):
                pt_ = pS_[h][i % 2]
                T.op("pe", lambda e: e.matmul(pt_[0:bs, qc:512], lhsT=KT[64 * h:64 * h + 64, k0:k0 + bs], rhs=QT[64 * h:64 * h + 64, q0 + qc:q0 + 512],
                                              start=True, stop=True), reads=["KT", "QT"], writes=["pS%d_%d" % (h, i % 2)])

        def emit_exp_pv(j):
            k0, bs, qc = blk_geom(j)
            i = cnt[0] + j
            jl = j - nfull
            for h in range(2):
                pt_ = pS_[h][i % 2]
                pb = Pb[h][i % NPB]
                pbn = "P%d_%d" % (h, i % NPB)
                T.op("act", lambda e: e.activation(out=pb[0:bs, qc:512], in_=pt_[0:bs, qc:512], func=AF.Exp, bias=bias[0:bs, h, j:j + 1], scale=SCALE),
                     reads=["pS%d_%d" % (h, i % 2), "bias"], writes=[pbn])
                if jl >= 0:
                    T.op("pool", lambda e: e.tensor_tensor(out=pb[:, qc:qc + 128], in0=pb[:, qc:qc + 128], in1=tri, op=ALU.mult), reads=[pbn, "cst"], writes=[pbn])
            for h in range(2):
                pb = Pb[h][i % NPB]
                pbn = "P%d_%d" % (h, i % NPB)
                T.op("pe", lambda e: e.matmul(pO[h][:, qc:512], lhsT=Vaug[0:bs, j, h, :], rhs=pb[0:bs, qc:512], start=(j == 0), stop=(j == nblocks - 1)),
                     reads=["Vaug", pbn], writes=["pO%d" % h])

        emit_qk(0)
        for j in range(nblocks):
            if j + 1 < nblocks:
                emit_qk(j + 1)
            emit_exp_pv(j)
        cnt[0] += nblocks
        T.op("dve", lambda e: e.reciprocal(out=rsum[64:65, :], in_=pO[0][64:65, :]), reads=["pO0"], writes=["rsum"])
        T.op("dve", lambda e: e.reciprocal(out=rsum[0:1, :], in_=pO[1][0:1, :]), reads=["pO1"], writes=["rsum"])
        T.op("pe", lambda e: e.matmul(pRB[:], lhsT=cst[:, C_SEL:C_SEL + 128], rhs=rsum[:], start=True, stop=True), reads=["cst", "rsum"], writes=["pRB"])
        T.op("act", lambda e: e.activation(out=Osb[0:64, :], in_=pO[0][0:64, :], func=AF.Copy), reads=["pO0"], writes=["Osb"])
        T.op("act", lambda e: e.activation(out=Osb[64:128, :], in_=pO[1][64:128, :], func=AF.Copy), reads=["pO1"], writes=["Osb"])
        T.op("dve", lambda e: e.tensor_tensor(out=Osb[:], in0=Osb[:], in1=pRB[:], op=ALU.mult), reads=["Osb", "pRB"], writes=["Osb"])
        c0 = 512 * (Q - 1)
        T.op("dve", lambda e: e.tensor_tensor(out=mixA[:], in0=Osb[:], in1=zaT[:, c0:c0 + 512], op=ALU.mult), reads=["Osb", "zaT"], writes=["mixA"])
        T.dma("sp", lambda e: e.dma_start(out=mix_src_ap(0, c0), in_=mixA[:]), reads=["mixA"], writes=["mix_src"])
        if (c0 + 512) % CH == 0:
            ci = c0 // CH
            T.cc(lambda e: e.collective_compute("AllGather", ALU.bypass, replica_groups=[[0, 1, 2, 3], [4, 5, 6, 7]],
                                                ins=[mix_src[ci * 256:(ci + 1) * 256, :]], outs=[mix_dst[ci * 1024:(ci + 1) * 1024, :]]),
                 reads=["mix_src"], writes=["mix_dst"])
    p2.close()
    if os.environ.get("K_STOP") == "p2":
        return early_exit()
    pa.close()
    T.barrier()

    p3 = contextlib.ExitStack()
    wob = sb(p3, "wob", [128, 8, D], BF16)
    gpost_bc = sb(p3, "gpost_bc", [128, D])
    qidx_sb = sb(p3, "qidx_sb", [128, 8], I32)
    mixT = sb(p3, "mixT", [128, 8, TQ], BF16)
    xres = [sb(p3, "xres%d" % i, [128, D]) for i in range(2)]
    ybuf = [sb(p3, "ybuf%d" % i, [128, D]) for i in range(2)]
    junk3 = sb(p3, "junk3", [128, 512], BF16)
    ss3 = sb(p3, "ss3", [128, 8])
    pY = [[ps(p3, "pY%d_%d" % (i, hh), [128, 512]) for hh in range(2)] for i in range(2)]
    T.dma("pool", lambda e: e.dma_start(out=wob[:], in_=wout.rearrange("(c p) n -> p c n", p=128)), writes=["wob"])
    T.dma("sp", lambda e: e.dma_start(out=gpost_bc[:], in_=gpost.partition_broadcast(128)), writes=["gpost_bc"])
    T.dma("sp", lambda e: e.dma_start(out=qidx_sb[:], in_=qidx[:, :]), writes=["qidx_sb"])
    mix_rows = mix_dst.ap().rearrange("r (q t) -> (r q) t", t=TQ)
    for kc in range(8):
        T.dma("pool", lambda e: e.indirect_dma_start(out=mixT[:, kc, :], out_offset=None, in_=mix_rows,
                                                     in_offset=bass.IndirectOffsetOnAxis(ap=qidx_sb[:, kc:kc + 1], axis=0)),
              reads=["qidx_sb", "mix_dst"], writes=["mixT"])
    for tb in range(TQ // 128):
        k = tb % 2
        xr, yb = xres[k], ybuf[k]
        T.dma("sp", lambda e: e.dma_start(out=xr[:], in_=xq[tb * 128:(tb + 1) * 128, :]), writes=["xres%d" % k])
        for hh in range(2):
            for kc in range(8):
                wrow = (kc // 2) + 4 * (kc % 2)
                T.op("pe", lambda e: e.matmul(pY[k][hh][:], lhsT=mixT[:, kc, tb * 128:(tb + 1) * 128], rhs=wob[:, wrow, hh * 512:(hh + 1) * 512],
                                              start=(kc == 0), stop=(kc == 7)), reads=["mixT", "wob"], writes=["pY%d_%d" % (k, hh)])
        for hh in range(2):
            T.op("act", lambda e: e.activation(out=junk3[:], in_=pY[k][hh][:], func=AF.Square, accum_out=ss3[:, hh:hh + 1]),
                 reads=["pY%d_%d" % (k, hh)], writes=["junk3", "ss3"])
        T.op("dve", lambda e: e.tensor_tensor(out=ss3[:, 2:3], in0=ss3[:, 0:1], in1=ss3[:, 1:2], op=ALU.add), reads=["ss3"], writes=["ss3"])
        T.op("act", lambda e: e.activation(out=ss3[:, 3:4], in_=ss3[:, 2:3], func=AF.Ln, bias=EPS, scale=1.0 / D), reads=["ss3"], writes=["ss3"])
        T.op("act", lambda e: e.activation(out=ss3[:, 4:5], in_=ss3[:, 3:4], func=AF.Exp, scale=-0.5), reads=["ss3"], writes=["ss3"])
        for hh in range(2):
            T.op("dve", lambda e: e.scalar_tensor_tensor(out=yb[:, hh * 512:(hh + 1) * 512], in0=pY[k][hh][:], scalar=ss3[:, 4:5],
                                                         in1=gpost_bc[:, hh * 512:(hh + 1) * 512], op0=ALU.mult, op1=ALU.mult),
                 reads=["pY%d_%d" % (k, hh), "ss3", "gpost_bc"], writes=["ybuf%d" % k])
        T.op("pool", lambda e: e.tensor_tensor(out=yb[:], in0=yb[:], in1=xr[:], op=ALU.add), reads=["ybuf%d" % k, "xres%d" % k], writes=["ybuf%d" % k])
        T.dma("sp", lambda e: e.dma_start(out=y_q[tb * 128:(tb + 1) * 128, :], in_=yb[:]), reads=["ybuf%d" % k])
    p3.close()
    T.barrier()

    SBI = min(4, cfg.SB)
    NGRP = cfg.SB // SBI
    NPG = cfg.NPG
    NG = NPG // 4
    IN_W = 4104
    O_QA, O_KA, O_VA, O_FA, O_ZA, O_QB, O_FB, O_VB, O_ZB = 0, 512, 1024, 1536, 1544, 2056, 2568, 3080, 3592
    if do_sample:
        xs_all = din("xs", [cfg.NS, D])
        state_all = din("state", [cfg.SB, 4, 128, 128])
        ptab_all = din("ptab", [1, cfg.SB * NPG], I32)
        ck_d = din("ck", [cfg.NPOOL * 32, 2048])
        cv_d = din("cv", [cfg.NPOOL * 32, 2048])
        clf_d = din("clf", [cfg.NPOOL * 32, 32])
        win_d = din("win", [D, IN_W])
        bf8_d = din("bf8", [1, 8])
        hlrows_d = din("hlrows", [1, 1024])
        hlT_d = din("hlT", [128, 8])
        gonrow_d = din("gonrow", [1, 128])
        cst2_d = din("cst2", [128, c2["END"]])
        ys_all = dout("ys", [cfg.NS, D])
        sk_all = dout("sk", [cfg.NS, 512])
        sv_all = dout("sv", [cfg.NS, 512])
        slf_all = dout("slf", [cfg.NS, 8])
        sS_all = dout("sS", [cfg.SB, 4, 128, 128])

    def sample_group(g):
        SB, NS = SBI, 4 * SBI
        sfx[0] = "_g%d" % g
        xs_d = xs_all[NS * g:NS * (g + 1), :]
        state_d = state_all[SB * g:SB * (g + 1)]
        ptab_d = ptab_all[:, SB * NPG * g:SB * NPG * (g + 1)]
        ys_o = ys_all[NS * g:NS * (g + 1), :]
        sk_o = sk_all[NS * g:NS * (g + 1), :]
        sv_o = sv_all[NS * g:NS * (g + 1), :]
        slf_o = slf_all[NS * g:NS * (g + 1), :]
        sS_o = sS_all[SB * g:SB * (g + 1)]

        sp_ = contextlib.ExitStack()
        cst2 = sb(sp_, "cst2", [128, c2["END"]])
        T.dma("sp", lambda e: e.dma_start(out=cst2[:], in_=cst2_d[:, :]), writes=["cst2"])
        BT = cst2[0:NS, c2["BT"]:c2["BT"] + NS]
        BU = cst2[0:NS, c2["BU"]:c2["BU"] + NS]
        BT128 = cst2[:, c2["BT"]:c2["BT"] + 128]
        BU128 = cst2[:, c2["BU"]:c2["BU"] + 128]
        MNEW = cst2[0:NS, c2["MNEW"]:c2["MNEW"] + SB * 32]
        BMT = cst2[:, c2["BMT"]:c2["BMT"] + SB * NS]
        BM = cst2[0:NS, c2["BM"]:c2["BM"] + SB]
        SUP = cst2[:, c2["SUP"]:c2["SUP"] + 128]
        SELP = cst2[:, c2["SELP"]:c2["SELP"] + 4]
        GCOL = cst2[:, c2["GCOL"]:c2["GCOL"] + 1]
        ONESF = cst[:, C_ONES:C_ONES + 128]

        u = sb(sp_, "u", [NS, IN_W])
        QaT = sb(sp_, "QaT", [128, 4, NS], BF16)
        KnT = sb(sp_, "KnT", [128, 4, NS], BF16)
        Qbd = sb(sp_, "Qbd", [128, 4, SB, 8], BF16)
        Q1pad = sb(sp_, "Q1pad", [128, 4, SB, NS], BF16)
        K3pad = sb(sp_, "K3pad", [NS, SB, 512], BF16)
        ATs = sb(sp_, "ATs", [NS, 4, NS], BF16)
        EbT = sb(sp_, "EbT", [128, 4, NS])
        vbs = sb(sp_, "vbs", [NS, 512], BF16)
        vas = sb(sp_, "vas", [NS, 512], BF16)
        Pnew = sb(sp_, "Pnew", [NS, SB, 32], BF16)
        idx_all = sb(sp_, "idx_all", [128, SB * NG], I32)
        mixs = sb(sp_, "mixs", [NS, D], BF16)
        oa_all = sb(sp_, "oa_all", [NS, 512])
        xs_sb = sb(sp_, "xs_sb", [NS, D])
        gpost_s = sb(sp_, "gpost_s", [NS, D])
        wob_s = sb(sp_, "wob_s", [128, 8, D], BF16)
        T.dma("sp", lambda e: e.dma_start(out=xs_sb[:], in_=xs_d[:, :]), writes=["xs_sb"])
        T.dma("sp", lambda e: e.dma_start(out=gpost_s[:], in_=gpost.partition_broadcast(NS)), writes=["gpost_s"])

        sa = contextlib.ExitStack()
        pt_i = sb(sa, "pt_i", [128, SB * NPG], I32)
        pt_f = sb(sa, "pt_f", [128, SB * NG, 4])
        idx_f = sb(sa, "idx_f", [128, SB * NG])
        T.dma("sp", lambda e: e.dma_start(out=pt_i[:], in_=ptab_d.partition_broadcast(128)), writes=["pt_i"])
        T.op("dve", lambda e: e.tensor_copy(out=pt_f[:].rearrange("p a b -> p (a b)"), in_=pt_i[:]), reads=["pt_i"], writes=["pt_f"])
        T.op("dve", lambda e: e.tensor_tensor(out=pt_f[:], in0=pt_f[:], in1=SELP.unsqueeze(1).to_broadcast([128, SB * NG, 4]), op=ALU.mult),
             reads=["pt_f", "cst2"], writes=["pt_f"])
        T.op("dve", lambda e: e.tensor_reduce(out=idx_f[:], in_=pt_f[:], axis=AX.X, op=ALU.add), reads=["pt_f"], writes=["idx_f"])
        T.op("dve", lambda e: e.tensor_scalar(out=idx_f[:], in0=idx_f[:], scalar1=32.0, scalar2=GCOL, op0=ALU.mult, op1=ALU.add),
             reads=["idx_f", "cst2"], writes=["idx_f"])
        T.op("dve", lambda e: e.tensor_copy(out=idx_all[:], in_=idx_f[:]), reads=["idx_f"], writes=["idx_all"])

        wTt = [sb(sa, "wTt%d" % i, [128, 8, 512], BF16) for i in range(2)]
        wst = [sb(sa, "wst%d" % i, [128, 8, 512]) for i in range(3)]
        wst_n = [0]

        def load_cast(dst, dst_name, src, w):
            i = wst_n[0] % 3
            wst_n[0] += 1
            st = wst[i]
            T.dma("sp", lambda e: e.dma_start(out=st[:, :, 0:w], in_=src.rearrange("(c p) n -> p c n", p=128)), writes=["wst%d" % i])
            T.op("act", lambda e: e.activation(out=dst, in_=st[:, :, 0:w], func=AF.Copy), reads=["wst%d" % i], writes=[dst_name])
        gpre_s = sb(sa, "gpre_s", [NS, D])
        hns = sb(sa, "hns", [NS, D], BF16)
        hnTs = sb(sa, "hnTs", [128, 8, NS], BF16)
        junks = sb(sa, "junks", [NS, D], BF16)
        sss = sb(sa, "sss", [NS, 8])
        T.dma("sp", lambda e: e.dma_start(out=gpre_s[:], in_=gpre.partition_broadcast(NS)), writes=["gpre_s"])
        T.op("act", lambda e: e.activation(out=junks[:], in_=xs_sb[:], func=AF.Square, accum_out=sss[:, 0:1]), reads=["xs_sb"], writes=["junks", "sss"])
        T.op("act", lambda e: e.activation(out=sss[:, 1:2], in_=sss[:, 0:1], func=AF.Ln, bias=EPS, scale=1.0 / D), reads=["sss"], writes=["sss"])
        T.op("act", lambda e: e.activation(out=sss[:, 2:3], in_=sss[:, 1:2], func=AF.Exp, scale=-0.5), reads=["sss"], writes=["sss"])
        T.op("dve", lambda e: e.scalar_tensor_tensor(out=hns[:], in0=xs_sb[:], scalar=sss[:, 2:3], in1=gpre_s[:], op0=ALU.mult, op1=ALU.mult),
             reads=["xs_sb", "sss", "gpre_s"], writes=["hns"])
        psA = [ps(sa, "psA%d" % i, [128, 512]) for i in range(2)]
        psB = [ps(sa, "psB%d" % i, [128, 8, NS]) for i in range(2)]
        psC = ps(sa, "psC", [128, 512])
        pTs = ps(sa, "pTs", [128, 8, 128], BF16)
        for kc in range(8):
            T.op("pe", lambda e: e.transpose(pTs[:, kc, 0:NS], hns[:, kc * 128:(kc + 1) * 128], identb[0:NS, 0:NS]), reads=["hns", "identb"], writes=["pTs"])
        T.op("act", lambda e: e.activation(out=hnTs[:], in_=pTs[:, :, 0:NS], func=AF.Copy), reads=["pTs"], writes=["hnTs"])
        chunks = [(O_QA, 512), (O_KA, 512), (O_VA, 512), (O_FA, 264), (O_FA + 264, 256), (O_QB, 512), (O_FB, 512), (O_VB, 512), (O_ZB, 512)]
        fmaj = {O_QA: (0, 0), O_KA: (0, 4), O_QB: (1, 0), O_FB: (1, 4)}
        for cg, (off, w) in enumerate(chunks):
            pa_ = psA[cg % 2]
            nm = "psA%d" % (cg % 2)
            wt = wTt[cg % 2]
            wtn = "wTt%d" % (cg % 2)
            load_cast(wt[:, :, 0:w], wtn, win_d[:, off:off + w], w)
            for kc in range(8):
                T.op("pe", lambda e: e.matmul(pa_[0:NS, 0:w], lhsT=hnTs[:, kc, :], rhs=wt[:, kc, 0:w], start=(kc == 0), stop=(kc == 7)),
                     reads=["hnTs", wtn], writes=[nm])
            if cg % 2 == 0:
                T.op("act", lambda e: e.activation(out=u[:, off:off + w], in_=pa_[0:NS, 0:w], func=AF.Copy), reads=[nm], writes=["u"])
            else:
                T.op("dve", lambda e: e.tensor_copy(out=u[:, off:off + w], in_=pa_[0:NS, 0:w]), reads=[nm], writes=["u"])
            if off in fmaj:
                g2, cb = fmaj[off]
                for ci in range(4):
                    for kc in range(8):
                        T.op("pe", lambda e: e.matmul(psB[g2][:, cb + ci, :], lhsT=wt[:, kc, ci * 128:(ci + 1) * 128], rhs=hnTs[:, kc, :], start=(kc == 0), stop=(kc == 7)),
                             reads=[wtn, "hnTs"], writes=["psB%d" % g2])
        T.dma("sp", lambda e: e.dma_start(out=sk_o[:, :], in_=u[:, O_KA:O_KA + 512]), reads=["u"])
        T.dma("sp", lambda e: e.dma_start(out=sv_o[:, :], in_=u[:, O_VA:O_VA + 512]), reads=["u"])
        T.op("act", lambda e: e.activation(out=QaT[:], in_=psB[0][:, 0:4, :], func=AF.Copy, scale=SCALE), reads=["psB0"], writes=["QaT"])
        T.op("act", lambda e: e.activation(out=KnT[:], in_=psB[0][:, 4:8, :], func=AF.Copy), reads=["psB0"], writes=["KnT"])
        T.op("pool", lambda e: e.memset(Qbd[:], 0.0), writes=["Qbd"])
        for half in range(2):
            T.op("dve", lambda e: e.tensor_copy(out=Qbd[64 * half:64 * half + 64, :, :, 4 * half:4 * half + 4],
                                                in_=QaT[64 * half:64 * half + 64, :, :].rearrange("p c (b q) -> p c b q", q=4)),
                 reads=["QaT"], writes=["Qbd"])
        qbTs = sb(sa, "qbTs", [128, 4, NS])
        gT = sb(sa, "gT", [128, 4, NS])
        lbT = sb(sa, "lbT", [128, 4])
        omlT = sb(sa, "omlT", [128, 4])
        hlT = sb(sa, "hlT", [128, 4, 2])
        hl2 = sb(sa, "hl2", [NS, 2, 512])
        lb_s = sb(sa, "lb_s", [NS, 512])
        oml_s = sb(sa, "oml_s", [NS, 512])
        gs = sb(sa, "gs", [NS, 512])
        loggs = sb(sa, "loggs", [128, 512])
        kbs = sb(sa, "kbs", [NS, 512])
        E3s = sb(sa, "E3s", [NS, 512])
        K3s = sb(sa, "K3s", [NS, 512], BF16)
        EnbT = sb(sa, "EnbT", [128, 4, NS])
        Q1T = sb(sa, "Q1T", [128, 4, NS], BF16)
        K2Ts = sb(sa, "K2Ts", [128, 4, NS], BF16)
        T.dma("sp", lambda e: e.dma_start(out=hlT[:].rearrange("p a b -> p (a b)"), in_=hlT_d[:, :]), writes=["hlT"])
        T.dma("sp", lambda e: e.dma_start(out=hl2[:].rearrange("p a b -> p (a b)"), in_=hlrows_d.partition_broadcast(NS)), writes=["hl2"])
        T.op("dve", lambda e: e.tensor_tensor(out=lbT[:], in0=hlT[:, :, 0], in1=hlT[:, :, 1], op=ALU.subtract), reads=["hlT"], writes=["lbT"])
        T.op("act", lambda e: e.activation(out=lbT[:], in_=lbT[:], func=AF.Sigmoid), reads=["lbT"], writes=["lbT"])
        T.op("dve", lambda e: e.tensor_scalar(out=omlT[:], in0=lbT[:], scalar1=-1.0, scalar2=1.0, op0=ALU.mult, op1=ALU.add), reads=["lbT"], writes=["omlT"])
        T.op("dve", lambda e: e.tensor_tensor(out=lb_s[:], in0=hl2[:, 0, :], in1=hl2[:, 1, :], op=ALU.subtract), reads=["hl2"], writes=["lb_s"])
        T.op("act", lambda e: e.activation(out=lb_s[:], in_=lb_s[:], func=AF.Sigmoid), reads=["lb_s"], writes=["lb_s"])
        T.op("dve", lambda e: e.tensor_scalar(out=oml_s[:], in0=lb_s[:], scalar1=-1.0, scalar2=1.0, op0=ALU.mult, op1=ALU.add), reads=["lb_s"], writes=["oml_s"])
        T.op("act", lambda e: e.activation(out=qbTs[:], in_=psB[1][:, 0:4, :], func=AF.Silu), reads=["psB1"], writes=["qbTs"])
        T.op("act", lambda e: e.activation(out=gT[:], in_=psB[1][:, 4:8, :], func=AF.Sigmoid), reads=["psB1"], writes=["gT"])
        T.op("dve", lambda e: e.tensor_tensor(out=gT[:], in0=gT[:], in1=omlT[:].unsqueeze(2).to_broadcast([128, 4, NS]), op=ALU.mult), reads=["gT", "omlT"], writes=["gT"])
        T.op("dve", lambda e: e.tensor_tensor(out=gT[:], in0=gT[:], in1=lbT[:].unsqueeze(2).to_broadcast([128, 4, NS]), op=ALU.add), reads=["gT", "lbT"], writes=["gT"])
        T.op("act", lambda e: e.activation(out=gs[:], in_=u[:, O_FB:O_FB + 512], func=AF.Sigmoid), reads=["u"], writes=["gs"])
        T.op("dve", lambda e: e.tensor_tensor(out=gs[:], in0=gs[:], in1=oml_s[:], op=ALU.mult), reads=["gs", "oml_s"], writes=["gs"])
        T.op("dve", lambda e: e.tensor_tensor(out=gs[:], in0=gs[:], in1=lb_s[:], op=ALU.add), reads=["gs", "lb_s"], writes=["gs"])
        T.op("pool", lambda e: e.memset(loggs[:], 0.0), writes=["loggs"])
        T.op("act", lambda e: e.activation(out=loggs[0:NS, :], in_=gs[:], func=AF.Ln), reads=["gs"], writes=["loggs"])
        T.op("dve", lambda e: e.tensor_scalar(out=kbs[:], in0=gs[:], scalar1=-1.0, scalar2=1.0, op0=ALU.mult, op1=ALU.add), reads=["gs"], writes=["kbs"])
        for h in range(4):
            T.op("pe", lambda e: e.matmul(psB[0][:, h, :], lhsT=loggs[:, h * 128:(h + 1) * 128], rhs=BT128[:, 0:NS], start=True, stop=True),
                 reads=["loggs", "cst2"], writes=["psB0"])
        T.op("pe", lambda e: e.matmul(psC[:, :], lhsT=BU128, rhs=loggs[:], start=True, stop=True), reads=["loggs", "cst2"], writes=["psC"])
        T.op("act", lambda e: e.activation(out=EbT[:], in_=psB[0][:, 0:4, :], func=AF.Exp), reads=["psB0"], writes=["EbT"])
        T.op("act", lambda e: e.activation(out=EnbT[:], in_=psB[0][:, 0:4, :], func=AF.Exp, scale=-1.0), reads=["psB0"], writes=["EnbT"])
        T.op("act", lambda e: e.activation(out=E3s[:], in_=psC[0:NS, :], func=AF.Exp), reads=["psC"], writes=["E3s"])
        T.op("dve", lambda e: e.tensor_tensor(out=Q1T[:], in0=qbTs[:], in1=EbT[:], op=ALU.mult), reads=["qbTs", "EbT"], writes=["Q1T"])
        T.op("dve", lambda e: e.tensor_scalar(out=gT[:], in0=gT[:], scalar1=-1.0, scalar2=1.0, op0=ALU.mult, op1=ALU.add), reads=["gT"], writes=["gT"])
        T.op("dve", lambda e: e.tensor_tensor(out=K2Ts[:], in0=gT[:], in1=EnbT[:], op=ALU.mult), reads=["gT", "EnbT"], writes=["K2Ts"])
        T.op("dve", lambda e: e.tensor_tensor(out=K3s[:], in0=kbs[:], in1=E3s[:], op=ALU.mult), reads=["kbs", "E3s"], writes=["K3s"])
        for b0 in range(0, SB, 4):
            nb4 = min(4, SB - b0)
            T.op("dve", lambda e: e.tensor_tensor(out=K3pad[:, b0:b0 + nb4, :], in0=K3s[:].unsqueeze(1).to_broadcast([NS, nb4, 512]),
                                                  in1=BM[:, b0:b0 + nb4].unsqueeze(2).to_broadcast([NS, nb4, 512]), op=ALU.mult), reads=["K3s", "cst2"], writes=["K3pad"])
        bmt3 = BMT.rearrange("p (b t) -> p b t", t=NS)
        for h in range(4):
            T.op("pool", lambda e: e.tensor_tensor(out=Q1pad[:, h, :, :], in0=Q1T[:, h, :].unsqueeze(1).to_broadcast([128, SB, NS]), in1=bmt3, op=ALU.mult),
                 reads=["Q1T", "cst2"], writes=["Q1pad"])
        T.op("dve", lambda e: e.tensor_copy(out=vbs[:], in_=u[:, O_VB:O_VB + 512]), reads=["u"], writes=["vbs"])
        T.op("dve", lambda e: e.tensor_copy(out=vas[:], in_=u[:, O_VA:O_VA + 512]), reads=["u"], writes=["vas"])
        for h in range(4):
            T.op("pe", lambda e: e.matmul(psB[1][0:NS, h, :], lhsT=K2Ts[:, h, :], rhs=Q1T[:, h, :], start=True, stop=True),
                 reads=["K2Ts", "Q1T"], writes=["psB1"])
        T.op("dve", lambda e: e.tensor_tensor(out=ATs[:], in0=psB[1][0:NS, 0:4, :], in1=BT.unsqueeze(1).to_broadcast([NS, 4, NS]), op=ALU.mult),
             reads=["psB1", "cst2"], writes=["ATs"])
        bf8s = sb(sa, "bf8s", [NS, 8])
        lfn = sb(sa, "lfn", [128, 8])
        ncn = sb(sa, "ncn", [NS, 8])
        snw = sb(sa, "snw", [NS, SB, 32])
        T.dma("sp", lambda e: e.dma_start(out=bf8s[:], in_=bf8_d.partition_broadcast(NS)), writes=["bf8s"])
        T.op("pool", lambda e: e.memset(lfn[:], 0.0), writes=["lfn"])
        T.op("dve", lambda e: e.tensor_tensor(out=lfn[0:NS, :], in0=u[:, O_FA:O_FA + 8], in1=bf8s[:], op=ALU.add), reads=["u", "bf8s"], writes=["lfn"])
        T.op("act", lambda e: e.activation(out=lfn[0:NS, :], in_=lfn[0:NS, :], func=AF.Exp, scale=-1.0), reads=["lfn"], writes=["lfn"])
        T.op("act", lambda e: e.activation(out=lfn[0:NS, :], in_=lfn[0:NS, :], func=AF.Ln, bias=1.0, scale=1.0), reads=["lfn"], writes=["lfn"])
        T.op("dve", lambda e: e.tensor_scalar(out=lfn[0:NS, :], in0=lfn[0:NS, :], scalar1=-1.0, scalar2=None, op0=ALU.mult), reads=["lfn"], writes=["lfn"])
        T.dma("sp", lambda e: e.dma_start(out=slf_o[:, :], in_=lfn[0:NS, :]), reads=["lfn"])
        T.op("pe", lambda e: e.matmul(psC[:, 0:8], lhsT=BT128, rhs=lfn[:], start=True, stop=True), reads=["lfn", "cst2"], writes=["psC"])
        T.op("dve", lambda e: e.tensor_scalar(out=ncn[:], in0=psC[0:NS, 0:8], scalar1=-1.0, scalar2=None, op0=ALU.mult), reads=["psC"], writes=["ncn"])
        pSn = psA[0][0:NS, :].rearrange("p (b c) -> p b c", c=32)[:, 0:SB, :]
        for hc in range(4):
            T.op("pe", lambda e: e.matmul(pSn[:, :, 8 * hc:8 * hc + 8], lhsT=KnT[:, hc, :], rhs=Qbd[:, hc, :, :], start=True, stop=True),
                 reads=["KnT", "Qbd"], writes=["psA0"])
        T.op("dve", lambda e: e.tensor_tensor(out=snw[:].rearrange("p b (h q) -> p b h q", q=4), in0=pSn.rearrange("p b (h q) -> p b h q", q=4),
                                              in1=ncn[:].unsqueeze(1).unsqueeze(3).to_broadcast([NS, SB, 8, 4]), op=ALU.add),
             reads=["psA0", "ncn"], writes=["snw"])
        T.op("act", lambda e: e.activation(out=snw[:], in_=snw[:], func=AF.Exp), reads=["snw"], writes=["snw"])
        T.op("dve", lambda e: e.tensor_tensor(out=Pnew[:], in0=snw[:], in1=MNEW.rearrange("p (b c) -> p b c", c=32), op=ALU.mult), reads=["snw", "cst2"], writes=["Pnew"])
        for hh in range(2):
            load_cast(wob_s[:, :, hh * 512:(hh + 1) * 512], "wob_s", wout[:, hh * 512:(hh + 1) * 512], 512)
        sa.close()
        T.barrier()

        sg = contextlib.ExitStack()
        Kt = [sb(sg, "Kt%d" % i, [128, NPG, 512], BF16) for i in range(2)]
        Vt = [sb(sg, "Vt%d" % i, [128, NPG, 512], BF16) for i in range(2)]
        Lt = [sb(sg, "Lt%d" % i, [128, NG, 4, 8]) for i in range(2)]
        Sf = [sb(sg, "Sf%d" % i, [128, 4, 128]) for i in range(2)]
        Sbf = [sb(sg, "Sbf%d" % i, [128, 4, 128], BF16) for i in range(2)]

        def issue_gather(bq):
            k = bq % 2
            for i in range(NG):
                col = bq * NG + i
                off = bass.IndirectOffsetOnAxis(ap=idx_all[:, col:col + 1], axis=0)
                T.dma("pool", lambda e: e.indirect_dma_start(out=Kt[k][:, 4 * i:4 * i + 4, :].rearrange("p a b -> p (a b)"), out_offset=None,
                                                             in_=ck_d[:, :], in_offset=off), reads=["idx_all"], writes=["Kt%d" % k])
                T.dma("pool", lambda e: e.indirect_dma_start(out=Vt[k][:, 4 * i:4 * i + 4, :].rearrange("p a b -> p (a b)"), out_offset=None,
                                                             in_=cv_d[:, :], in_offset=off), reads=["idx_all"], writes=["Vt%d" % k])
                T.dma("pool", lambda e: e.indirect_dma_start(out=Lt[k][:, i, :, :].rearrange("p a b -> p (a b)"), out_offset=None,
                                                             in_=clf_d[:, :], in_offset=off), reads=["idx_all"], writes=["Lt%d" % k])
            T.dma("sp", lambda e: e.dma_start(out=Sf[k][:], in_=state_d[bq].rearrange("h k v -> k h v")), writes=["Sf%d" % k])

        def cast_state(bq):
            k = bq % 2
            T.op("act", lambda e: e.activation(out=Sbf[k][:], in_=Sf[k][:], func=AF.Copy), reads=["Sf%d" % k], writes=["Sbf%d" % k])

        issue_gather(0)
        cast_state(0)

        sl = contextlib.ExitStack()
        KTs = sb(sl, "KTs", [128, NPG, 4, 128], BF16)
        Wl = sb(sl, "Wl", [128, NG, 8])
        V3 = sb(sl, "V3", [128, NG, 4, 8])
        Rl = sb(sl, "Rl", [128, NG, 4, 8])
        scb = sb(sl, "scb", [128, NPG, 32])
        Pp = sb(sl, "Pp", [128, NPG, 32], BF16)
        Psm = sb(sl, "Psm", [128, 32])
        rs_ = sb(sl, "rs_", [4, 8])
        oab = [sb(sl, "oab%d" % i, [4, 512]) for i in range(2)]
        Snew = [sb(sl, "Snew%d" % i, [128, 4, 128]) for i in range(2)]
        pTr = [ps(sl, "pTr%d" % i, [128, 8, 128], BF16) for i in range(2)]
        pSc_full = ps(sl, "pSc", [128, 16, 32])
        pSc = pSc_full[:, 0:NPG, :]
        pR1 = ps(sl, "pR1", [128, NG, 8])
        pOs = ps(sl, "pOs", [4, 8, 64])
        pSm = ps(sl, "pSm", [4, 8])
        pOh = ps(sl, "pOh", [NS, 4, 128])
        pSU = ps(sl, "pSU", [128, 4, 128])
        for h in range(4):
            T.op("pe", lambda e: e.matmul(pOh[:, h, :], lhsT=ATs[:, h, :], rhs=vbs[:, h * 128:(h + 1) * 128], start=(h == 0), stop=False, skip_group_check=True),
                 reads=["ATs", "vbs"], writes=["pOh"])
        tcount = [0]
        for bq in range(SB):
            k = bq % 2
            if bq + 1 < SB:
                issue_gather(bq + 1)
            for s2 in range(NPG // 2):
                pt2 = pTr[tcount[0] % 2]
                nm = "pTr%d" % (tcount[0] % 2)
                for sl_ in range(2):
                    slot = 2 * s2 + sl_
                    for hc in range(4):
                        T.op("pe", lambda e: e.transpose(pt2[:, 4 * sl_ + hc, :], Kt[k][:, slot, hc * 128:(hc + 1) * 128], identb[:]),
                             reads=["Kt%d" % k, "identb"], writes=[nm])
                dst = KTs[:, 2 * s2:2 * s2 + 2, :, :].rearrange("p a b c -> p (a b) c")
                T.op("act", lambda e: e.activation(out=dst, in_=pt2[:], func=AF.Copy), reads=[nm], writes=["KTs"])
                tcount[0] += 1
            for slot in range(NPG):
                for hc in range(4):
                    T.op("pe", lambda e: e.matmul(pSc[:, slot, 8 * hc:8 * hc + 8], lhsT=KTs[:, slot, hc, :], rhs=Qbd[:, hc, bq, :], start=True, stop=True),
                         reads=["KTs", "Qbd"], writes=["pSc"])
            T.op("dve", lambda e: e.tensor_reduce(out=Wl[:], in_=Lt[k][:].rearrange("p i t h -> p i h t"), axis=AX.X, op=ALU.add), reads=["Lt%d" % k], writes=["Wl"])
            first = True
            for i in range(NG):
                T.op("pe", lambda e: e.matmul(pR1[:, i, :], lhsT=SUP, rhs=Wl[:, i, :], start=first, stop=False, skip_group_check=True), reads=["cst2", "Wl"], writes=["pR1"])
                first = False
                for i2 in range(i + 1, NG):
                    T.op("pe", lambda e: e.matmul(pR1[:, i, :], lhsT=ONESF, rhs=Wl[:, i2, :], start=False, stop=False, skip_group_check=True),
                         reads=["cst", "Wl"], writes=["pR1"])
            T.op("dve", lambda e: e.memset(V3[:, :, 3, :], 0.0), writes=["V3"])
            T.op("dve", lambda e: e.tensor_copy(out=V3[:, :, 2, :], in_=Lt[k][:, :, 3, :]), reads=["Lt%d" % k], writes=["V3"])
            T.op("dve", lambda e: e.tensor_tensor(out=V3[:, :, 1, :], in0=V3[:, :, 2, :], in1=Lt[k][:, :, 2, :], op=ALU.add), reads=["Lt%d" % k, "V3"], writes=["V3"])
            T.op("dve", lambda e: e.tensor_tensor(out=V3[:, :, 0, :], in0=V3[:, :, 1, :], in1=Lt[k][:, :, 1, :], op=ALU.add), reads=["Lt%d" % k, "V3"], writes=["V3"])
            T.op("dve", lambda e: e.tensor_tensor(out=Rl[:], in0=V3[:], in1=pR1[:].unsqueeze(2).to_broadcast([128, NG, 4, 8]), op=ALU.add),
                 reads=["V3", "pR1"], writes=["Rl"])
            T.op("dve", lambda e: e.tensor_tensor(out=scb[:].rearrange("p s (h q) -> p s h q", q=4), in0=pSc.rearrange("p s (h q) -> p s h q", q=4),
                                                  in1=Rl[:].rearrange("p i t h -> p (i t) h").unsqueeze(3).to_broadcast([128, NPG, 8, 4]), op=ALU.add),
                 reads=["pSc", "Rl"], writes=["scb"])
            T.op("act", lambda e: e.activation(out=Pp[:], in_=scb[:], func=AF.Exp), reads=["scb"], writes=["Pp"])
            if bq + 1 < SB:
                cast_state(bq + 1)
            T.op("dve", lambda e: e.tensor_reduce(out=Psm[:], in_=Pp[:].rearrange("p s c -> p c s"), axis=AX.X, op=ALU.add), reads=["Pp"], writes=["Psm"])
            first = True
            for slot in range(NPG):
                for h in range(8):
                    T.op("pe", lambda e: e.matmul(pOs[:, h, :], lhsT=Pp[:, slot, 4 * h:4 * h + 4], rhs=Vt[k][:, slot, h * 64:(h + 1) * 64],
                                                  start=first, stop=False, skip_group_check=True), reads=["Pp", "Vt%d" % k], writes=["pOs"])
                    first = False
            for h in range(8):
                T.op("pe", lambda e: e.matmul(pOs[:, h, :], lhsT=Pnew[:, bq, 4 * h:4 * h + 4], rhs=vas[:, h * 64:(h + 1) * 64],
                                              start=False, stop=(h == 7), skip_group_check=True), reads=["Pnew", "vas"], writes=["pOs"])
            for h in range(8):
                T.op("pe", lambda e: e.matmul(pSm[:, h:h + 1], lhsT=Psm[:, 4 * h:4 * h + 4], rhs=ONESF[:, 0:1], start=(h == 0), stop=False, skip_group_check=True),
                     reads=["Psm", "cst"], writes=["pSm"])
            for h in range(8):
                T.op("pe", lambda e: e.matmul(pSm[:, h:h + 1], lhsT=Pnew[:, bq, 4 * h:4 * h + 4], rhs=onesb[0:NS, 0:1], start=False, stop=(h == 7), skip_group_check=True),
                     reads=["Pnew", "onesb"], writes=["pSm"])
            T.op("dve", lambda e: e.reciprocal(out=rs_[:], in_=pSm[:]), reads=["pSm"], writes=["rs_"])
            T.op("dve", lambda e: e.tensor_tensor(out=oab[k][:].rearrange("p (h d) -> p h d", d=64), in0=pOs[:], in1=rs_[:].unsqueeze(2).to_broadcast([4, 8, 64]), op=ALU.mult),
                 reads=["pOs", "rs_"], writes=["oab%d" % k])
            T.dma("sp", lambda e: e.dma_start(out=oa_all[4 * bq:4 * bq + 4, :], in_=oab[k][:]), reads=["oab%d" % k], writes=["oa_all"])
            for h in range(4):
                T.op("pe", lambda e: e.matmul(pOh[:, h, :], lhsT=Q1pad[:, h, bq, :], rhs=Sbf[k][:, h, :], start=False, stop=(bq == SB - 1 and h == 3), skip_group_check=True),
                     reads=["Q1pad", "Sbf%d" % k], writes=["pOh"])
            for h in range(4):
                T.op("pe", lambda e: e.matmul(pSU[:, h, :], lhsT=K3pad[:, bq, h * 128:(h + 1) * 128], rhs=vbs[:, h * 128:(h + 1) * 128], start=True, stop=True),
                     reads=["K3pad", "vbs"], writes=["pSU"])
            for h in range(4):
                T.op("dve", lambda e: e.scalar_tensor_tensor(out=Snew[k][:, h, :], in0=Sf[k][:, h, :], scalar=EbT[:, h, 4 * bq + 3:4 * bq + 4], in1=pSU[:, h, :],
                                                             op0=ALU.mult, op1=ALU.add), reads=["Sf%d" % k, "EbT", "pSU"], writes=["Snew%d" % k])
            T.dma("sp", lambda e: e.dma_start(out=sS_o[bq].rearrange("h k v -> k h v"), in_=Snew[k][:]), reads=["Snew%d" % k])

        so = contextlib.ExitStack()
        zsil = sb(so, "zsil", [NS, 1024])
        ssh = sb(so, "ssh", [NS, 12])
        junk4 = sb(so, "junk4", [NS, 512], BF16)
        ob_s = sb(so, "ob_s", [NS, 4, 128])
        gon_s = sb(so, "gon_s", [NS, 128])
        mixTs = sb(so, "mixTs", [128, 8, NS], BF16)
        ysb = sb(so, "ysb", [NS, D])
        T.dma("sp", lambda e: e.dma_start(out=gon_s[:], in_=gonrow_d.partition_broadcast(NS)), writes=["gon_s"])
        T.op("act", lambda e: e.activation(out=zsil[:, 0:512], in_=u[:, O_ZA:O_ZA + 512], func=AF.Silu), reads=["u"], writes=["zsil"])
        T.op("act", lambda e: e.activation(out=zsil[:, 512:1024], in_=u[:, O_ZB:O_ZB + 512], func=AF.Silu), reads=["u"], writes=["zsil"])
        T.op("dve", lambda e: e.tensor_tensor(out=mixs[:, 0:512], in0=oa_all[:], in1=zsil[:, 0:512], op=ALU.mult), reads=["oa_all", "zsil"], writes=["mixs"])
        for h in range(4):
            T.op("act", lambda e: e.activation(out=junk4[:, 0:128], in_=pOh[:, h, :], func=AF.Square, accum_out=ssh[:, h:h + 1]), reads=["pOh"], writes=["junk4", "ssh"])
        T.op("act", lambda e: e.activation(out=ssh[:, 4:8], in_=ssh[:, 0:4], func=AF.Ln, bias=EPS, scale=1.0 / 128), reads=["ssh"], writes=["ssh"])
        T.op("act", lambda e: e.activation(out=ssh[:, 8:12], in_=ssh[:, 4:8], func=AF.Exp, scale=-0.5), reads=["ssh"], writes=["ssh"])
        T.op("dve", lambda e: e.tensor_tensor(out=ob_s[:], in0=pOh[:], in1=ssh[:, 8:12].unsqueeze(2).to_broadcast([NS, 4, 128]), op=ALU.mult), reads=["pOh", "ssh"], writes=["ob_s"])
        T.op("dve", lambda e: e.tensor_tensor(out=ob_s[:], in0=ob_s[:], in1=gon_s[:].unsqueeze(1).to_broadcast([NS, 4, 128]), op=ALU.mult), reads=["ob_s", "gon_s"], writes=["ob_s"])
        T.op("dve", lambda e: e.tensor_tensor(out=mixs[:, 512:1024], in0=ob_s[:].rearrange("p h v -> p (h v)"), in1=zsil[:, 512:1024], op=ALU.mult),
             reads=["ob_s", "zsil"], writes=["mixs"])
        for kc in range(8):
            T.op("pe", lambda e: e.transpose(pTr[0][:, kc, 0:NS], mixs[:, kc * 128:(kc + 1) * 128], identb[0:NS, 0:NS]), reads=["mixs", "identb"], writes=["pTr0"])
        T.op("act", lambda e: e.activation(out=mixTs[:], in_=pTr[0][:, :, 0:NS], func=AF.Copy), reads=["pTr0"], writes=["mixTs"])
        pY2 = [pSU[:].rearrange("p h v -> p (h v)"), pSc_full[:].rearrange("p s c -> p (s c)")]
        pY2n = ["pSU", "pSc"]
        for hh in range(2):
            for kc in range(8):
                T.op("pe", lambda e: e.matmul(pY2[hh][0:NS, :], lhsT=mixTs[:, kc, :], rhs=wob_s[:, kc, hh * 512:(hh + 1) * 512], start=(kc == 0), stop=(kc == 7)),
                     reads=["mixTs", "wob_s"], writes=[pY2n[hh]])
        for hh in range(2):
            T.op("act", lambda e: e.activation(out=junk4[:], in_=pY2[hh][0:NS, :], func=AF.Square, accum_out=ssh[:, hh:hh + 1]), reads=[pY2n[hh]], writes=["junk4", "ssh"])
        T.op("dve", lambda e: e.tensor_tensor(out=ssh[:, 2:3], in0=ssh[:, 0:1], in1=ssh[:, 1:2], op=ALU.add), reads=["ssh"], writes=["ssh"])
        T.op("act", lambda e: e.activation(out=ssh[:, 3:4], in_=ssh[:, 2:3], func=AF.Ln, bias=EPS, scale=1.0 / D), reads=["ssh"], writes=["ssh"])
        T.op("act", lambda e: e.activation(out=ssh[:, 4:5], in_=ssh[:, 3:4], func=AF.Exp, scale=-0.5), reads=["ssh"], writes=["ssh"])
        for hh in range(2):
            T.op("dve", lambda e: e.scalar_tensor_tensor(out=ysb[:, hh * 512:(hh + 1) * 512], in0=pY2[hh][0:NS, :], scalar=ssh[:, 4:5],
                                                         in1=gpost_s[:, hh * 512:(hh + 1) * 512], op0=ALU.mult, op1=ALU.mult),
                 reads=[pY2n[hh], "ssh", "gpost_s"], writes=["ysb"])
        T.op("dve", lambda e: e.tensor_tensor(out=ysb[:], in0=ysb[:], in1=xs_sb[:], op=ALU.add), reads=["ysb", "xs_sb"], writes=["ysb"])
        T.dma("sp", lambda e: e.dma_start(out=ys_o[:, :], in_=ysb[:]), reads=["ysb"])
        so.close()
        sl.close()
        sg.close()
        sp_.close()
        T.barrier()

    if do_sample:
        for g in range(NGRP):
            sample_group(g)

    T.finish("sp")
    T.close()
    es.close()
    return nc


W_OFF = {"qa": 0, "ka": 512, "va": 1024, "fa": 1536, "za": 1544, "qb": 2056, "fb": 2568, "vb": 3080, "zb": 3592}


def core_inputs(cfg, c, x_prompt, meta_tokens, w_in, b_forget, hgrn_lower_bound, hgrn_out_norm, pre_norm, post_norm, w_out, consts):
    b, j = c // 4, c % 4
    w = w_in[0]
    sl = lambda name: w[:, W_OFF[name] + 128 * j: W_OFF[name] + 128 * j + 128]
    wcore = np.concatenate([sl("qa"), sl("ka"), sl("za"), sl("qb"), sl("zb"), sl("va"),
                            w[:, W_OFF["fa"] + 2 * j: W_OFF["fa"] + 2 * j + 2], sl("fb"), sl("vb")], axis=1)
    CH = min(2048, cfg.SEQ)
    ci, qin, nq = (cfg.TQ * j) // CH, ((cfg.TQ * j) % CH) // cfg.TQ, CH // cfg.TQ
    qidx = ((ci * 1024 + np.arange(8)[None, :] * 128 + np.arange(128)[:, None]) * nq + qin).astype(np.int32)
    return {
        "xp": np.ascontiguousarray(x_prompt[b]),
        "xq": np.ascontiguousarray(x_prompt[b, cfg.TQ * j: cfg.TQ * (j + 1)]),
        "meta": np.ascontiguousarray(meta_tokens),
        "wcore": np.ascontiguousarray(wcore),
        "wout": np.ascontiguousarray(w_out[0]),
        "bf2": np.ascontiguousarray(b_forget[0, 2 * j: 2 * j + 2].reshape(1, 2)),
        "hl_row": np.ascontiguousarray(hgrn_lower_bound[:, 128 * j: 128 * j + 128].reshape(1, 256)),
        "gon": np.ascontiguousarray(hgrn_out_norm[0].reshape(128, 1)),
        "gpre": np.ascontiguousarray(pre_norm[0].reshape(1, D)),
        "gpost": np.ascontiguousarray(post_norm[0].reshape(1, D)),
        "cst": consts,
        "qidx": qidx,
    }


def assemble_prompt(cfg, res):
    B = 2
    L, SEQ = cfg.L, cfg.SEQ
    y = np.zeros((B, SEQ, D), np.float32)
    pk = np.zeros((1, B, L, 8, 64), np.float32)
    pv = np.zeros((1, B, L, 8, 64), np.float32)
    plf = np.zeros((1, B, L, 8), np.float32)
    pS = np.zeros((1, B, 4, 128, 128), np.float32)
    for c in range(8):
        b, j = c // 4, c % 4
        r = res[c]
        y[b, cfg.TQ * j: cfg.TQ * (j + 1)] = r["y_q"]
        pk[0, b, :, 2 * j: 2 * j + 2, :] = r["pkT"].T.reshape(L, 2, 64)
        pv[0, b, :, 2 * j: 2 * j + 2, :] = r["pv"].reshape(L, 2, 64)
        plf[0, b, :, 2 * j: 2 * j + 2] = r["plf"]
        pS[0, b, j] = r["pS"]
    return y, pk, pv, plf, pS


def core_inputs_sample(cfg, c, x_sample, cache_k, cache_v, cache_logf, state_hgrn, page_table, w_in, b_forget,
                       hgrn_lower_bound, hgrn_out_norm, consts2):
    SB = cfg.SB
    lo, hi = SB * c, SB * (c + 1)
    return {
        "xs": np.ascontiguousarray(x_sample[lo:hi].reshape(cfg.NS, D)),
        "state": np.ascontiguousarray(state_hgrn[0, lo:hi]),
        "ptab": np.ascontiguousarray(page_table[lo:hi].reshape(1, SB * cfg.NPG).astype(np.int32)),
        "ck": cache_k[0].reshape(cfg.NPOOL * 32, 2048),
        "cv": cache_v[0].reshape(cfg.NPOOL * 32, 2048),
        "clf": cache_logf[0].reshape(cfg.NPOOL * 32, 32),
        "win": np.ascontiguousarray(w_in[0]),
        "bf8": np.ascontiguousarray(b_forget[0].reshape(1, 8)),
        "hlrows": np.ascontiguousarray(hgrn_lower_bound.reshape(1, 1024)),
        "hlT": np.ascontiguousarray(hgrn_lower_bound.reshape(2, 4, 128).transpose(2, 1, 0).reshape(128, 8)),
        "gonrow": np.ascontiguousarray(hgrn_out_norm[0].reshape(1, 128)),
        "cst2": consts2,
    }


def assemble_sample(cfg, res):
    SB = cfg.SB
    ys = np.concatenate([res[c]["ys"].reshape(SB, 4, D) for c in range(8)], axis=0)
    sk = np.concatenate([res[c]["sk"].reshape(SB, 4, 8, 64) for c in range(8)], axis=0)[None]
    sv = np.concatenate([res[c]["sv"].reshape(SB, 4, 8, 64) for c in range(8)], axis=0)[None]
    slf = np.concatenate([res[c]["slf"].reshape(SB, 4, 8) for c in range(8)], axis=0)[None]
    sS = np.concatenate([res[c]["sS"] for c in range(8)], axis=0)[None]
    return ys, sk, sv, slf, sS


_PROG = {}


def run_all(cfg, inp):
    key = (cfg.NB, cfg.SB, cfg.NPG, cfg.NPOOL)
    if key not in _PROG:
        _PROG[key] = build_program(cfg, do_sample=True)
    nc = _PROG[key]
    consts, consts2 = make_consts(), make_consts2(Cfg(nb=cfg.NB, sb=min(4, cfg.SB), npg=cfg.NPG, npool=cfg.NPOOL))
    ins = []
    for c in range(8):
        d = core_inputs(cfg, c, inp["x_prompt"], inp["meta_tokens"], inp["w_in"], inp["b_forget"], inp["hgrn_lower_bound"],
                        inp["hgrn_out_norm"], inp["pre_norm"], inp["post_norm"], inp["w_out"], consts)
        d.update(core_inputs_sample(cfg, c, inp["x_sample"], inp["cache_k"], inp["cache_v"], inp["cache_logf"], inp["state_hgrn"],
                                    inp["page_table"], inp["w_in"], inp["b_forget"], inp["hgrn_lower_bound"], inp["hgrn_out_norm"], consts2))
        ins.append(d)
    res = run_bass_kernel_spmd(nc, ins, core_ids=list(range(8))).results
    y, pk, pv, plf, pS = assemble_prompt(cfg, res)
    ys, sk, sv, slf, sS = assemble_sample(cfg, res)
    return (y, ys, pk, pv, plf, pS, sk, sv, slf, sS)


def kernel(x_prompt, x_sample, cache_k, cache_v, cache_logf, state_hgrn, page_table, meta_tokens, w_in, b_forget,
           hgrn_lower_bound, hgrn_out_norm, pre_norm, post_norm, w_out):
    inp = dict(x_prompt=x_prompt, x_sample=x_sample, cache_k=cache_k, cache_v=cache_v, cache_logf=cache_logf,
               state_hgrn=state_hgrn, page_table=page_table, meta_tokens=meta_tokens, w_in=w_in, b_forget=b_forget,
               hgrn_lower_bound=hgrn_lower_bound, hgrn_out_norm=hgrn_out_norm, pre_norm=pre_norm, post_norm=post_norm, w_out=w_out)
    inp = {k: np.asarray(v) for k, v in inp.items()}
    cfg = Cfg(nb=x_prompt.shape[1] // 128, sb=x_sample.shape[0] // 8, npg=page_table.shape[1], npool=cache_k.shape[1])
    return run_all(cfg, inp)
```
